# Optimizing a Trainium2 kernel written in Bass

```python
import math
import jax
import jax.numpy as jnp
from jax import lax
import numpy as np

D_MODEL = 2048
BATCH = 8
SEQ = 4096
DEPTH = 4

N_MIXERS = 2
N_LAYERS_A = (DEPTH + 1) // 2
N_LAYERS_B = DEPTH // 2
CHUNK = 128
SGU_WIDTH = D_MODEL
SGU_GROUP = 128
SGU_HEADS = SGU_WIDTH // SGU_GROUP
SSM_WIDTH = D_MODEL
SSM_GROUP = 16
SSM_HEADS = SSM_WIDTH // SSM_GROUP
SSM_STATE = 64
DT_MIN = 1e-3
DT_MAX = 1e-1
FFN_HIDDEN = 5632
CONV_WIDTH = 3
EPS = 1e-6

kernel_name = 'hybrid_sgu_s5_convffn'


def rms_norm(x, g):
    xf = x.astype(jnp.float32)
    y = xf * lax.rsqrt(jnp.mean(xf * xf, axis=-1, keepdims=True) + EPS)
    return (y * g.astype(jnp.float32)).astype(x.dtype)


def chunked_sgu_mixer(h, w_in, g_v, w_s, b_s, w_out):
    bsz, seq, _ = h.shape
    z = jax.nn.gelu(h @ w_in)
    u, v = jnp.split(z, 2, axis=-1)
    v = rms_norm(v, g_v).reshape(bsz, seq // CHUNK, CHUNK, SGU_HEADS, SGU_GROUP)
    causal = jnp.tril(jnp.ones((CHUNK, CHUNK), dtype=bool))
    w = jnp.where(causal[None], w_s, jnp.zeros((), w_s.dtype))
    s = jnp.einsum('hts,bcshd->bcthd', w, v) + b_s.T[:, :, None]
    s = s.reshape(bsz, seq, SGU_WIDTH)
    return (u * s) @ w_out


def _cmul(ar, ai, br, bi):
    return ar * br - ai * bi, ar * bi + ai * br


def _scan_combine(earlier, later):
    a1r, a1i, b1r, b1i = earlier
    a2r, a2i, b2r, b2i = later
    ar, ai = _cmul(a2r, a2i, a1r, a1i)
    br, bi = _cmul(a2r, a2i, b1r, b1i)
    return ar, ai, br + b2r, bi + b2i


def s5_mixer(h, w_in, a_re, a_im, log_dt, b_re, b_im, c_re, c_im, d_skip, w_glu):
    f32 = jnp.float32
    bsz, seq, _ = h.shape
    n_chunks = seq // CHUNK
    u = (h @ w_in).astype(f32).reshape(bsz, n_chunks, CHUNK, SSM_HEADS, SSM_GROUP)
    u = u.transpose(1, 0, 2, 3, 4)
    dt = jnp.exp(log_dt.astype(f32))[:, None]
    lr, li = a_re.astype(f32), a_im.astype(f32)
    mag = jnp.exp(dt * lr)
    abar_r, abar_i = mag * jnp.cos(dt * li), mag * jnp.sin(dt * li)
    den = lr * lr + li * li
    qr = ((abar_r - 1.0) * lr + abar_i * li) / den
    qi = (abar_i * lr - (abar_r - 1.0) * li) / den
    bbar_r, bbar_i = _cmul(qr[..., None], qi[..., None], b_re.astype(f32), b_im.astype(f32))
    cr, ci = c_re.astype(f32), c_im.astype(f32)
    dd = d_skip.astype(f32).reshape(SSM_HEADS, SSM_GROUP)
    a_seq_r = jnp.broadcast_to(abar_r, (bsz, CHUNK, SSM_HEADS, SSM_STATE))
    a_seq_i = jnp.broadcast_to(abar_i, (bsz, CHUNK, SSM_HEADS, SSM_STATE))

    def chunk_step(carry, u_c):
        h0r, h0i = carry
        bur = jnp.einsum('gpc,btgc->btgp', bbar_r, u_c)
        bui = jnp.einsum('gpc,btgc->btgp', bbar_i, u_c)
        pr, pim, hr, hi = lax.associative_scan(
            _scan_combine, (a_seq_r, a_seq_i, bur, bui), axis=1)
        sr, si = _cmul(pr, pim, h0r[:, None], h0i[:, None])
        hr = hr + sr
        hi = hi + si
        y = (jnp.einsum('gcp,btgp->btgc', cr, hr)
             - jnp.einsum('gcp,btgp->btgc', ci, hi)
             + dd * u_c)
        return (hr[:, -1], hi[:, -1]), y

    init = (jnp.zeros((bsz, SSM_HEADS, SSM_STATE), f32),
            jnp.zeros((bsz, SSM_HEADS, SSM_STATE), f32))
    _, y = lax.scan(chunk_step, init, u)
    y = y.transpose(1, 0, 2, 3, 4).reshape(bsz, seq, SSM_WIDTH).astype(h.dtype)
    ga, gb = jnp.split(jax.nn.gelu(y) @ w_glu, 2, axis=-1)
    return ga * jax.nn.sigmoid(gb)


def conv_glu_ffn(h, w_up, conv_w, conv_b, w_down):
    seq = h.shape[1]
    z = h @ w_up
    zp = jnp.pad(z, ((0, 0), (CONV_WIDTH - 1, 0), (0, 0)))
    acc = conv_b + conv_w[CONV_WIDTH - 1] * zp[:, CONV_WIDTH - 1:CONV_WIDTH - 1 + seq]
    for k in range(CONV_WIDTH - 1):
        acc = acc + conv_w[k] * zp[:, k:k + seq]
    gate, val = jnp.split(acc, 2, axis=-1)
    return (jax.nn.silu(gate) * val) @ w_down


def setup_inputs(seed: int = 0) -> dict:
    key = jax.random.key(seed)
    ks = jax.random.split(key, 24)
    f32 = jnp.float32
    d = D_MODEL
    na, nb = N_LAYERS_A, N_LAYERS_B

    def nrm(k, shape, scale):
        return jax.random.normal(k, shape, f32) * scale

    def gain(k, shape):
        return 1.0 + 0.01 * jax.random.normal(k, shape, f32)

    n_idx = jnp.arange(SSM_STATE, dtype=f32)
    return {
        'x': nrm(ks[0], (BATCH, SEQ, d), 1.0),
        'norm_mix_g': gain(ks[1], (DEPTH, d)),
        'norm_ffn_g': gain(ks[2], (DEPTH, d)),
        'a_w_in': nrm(ks[3], (na, d, 2 * SGU_WIDTH), d ** -0.5),
        'a_g_v': gain(ks[4], (na, SGU_WIDTH)),
        'a_w_s': nrm(ks[5], (na, SGU_HEADS, CHUNK, CHUNK), 0.5 * CHUNK ** -0.5),
        'a_b_s': gain(ks[6], (na, SGU_HEADS, CHUNK)),
        'a_w_out': nrm(ks[7], (na, SGU_WIDTH, d), SGU_WIDTH ** -0.5),
        'b_w_in': nrm(ks[8], (nb, d, SSM_WIDTH), d ** -0.5),
        'b_a_re': -0.5 + nrm(ks[9], (nb, SSM_HEADS, SSM_STATE), 0.01),
        'b_a_im': math.pi * n_idx + nrm(ks[10], (nb, SSM_HEADS, SSM_STATE), 0.01),
        'b_log_dt': jax.random.uniform(ks[11], (nb, SSM_HEADS), f32,
                                       minval=math.log(DT_MIN), maxval=math.log(DT_MAX)),
        'b_b_re': nrm(ks[12], (nb, SSM_HEADS, SSM_STATE, SSM_GROUP), (2 * SSM_GROUP) ** -0.5),
        'b_b_im': nrm(ks[13], (nb, SSM_HEADS, SSM_STATE, SSM_GROUP), (2 * SSM_GROUP) ** -0.5),
        'b_c_re': nrm(ks[14], (nb, SSM_HEADS, SSM_GROUP, SSM_STATE), (2 * SSM_STATE) ** -0.5),
        'b_c_im': nrm(ks[15], (nb, SSM_HEADS, SSM_GROUP, SSM_STATE), (2 * SSM_STATE) ** -0.5),
        'b_d': nrm(ks[16], (nb, SSM_WIDTH), 1.0),
        'b_w_glu': nrm(ks[17], (nb, SSM_WIDTH, 2 * d), SSM_WIDTH ** -0.5),
        'f_w_up': nrm(ks[18], (DEPTH, d, 2 * FFN_HIDDEN), d ** -0.5),
        'f_conv_w': nrm(ks[19], (DEPTH, CONV_WIDTH, 2 * FFN_HIDDEN), CONV_WIDTH ** -0.5),
        'f_conv_b': nrm(ks[20], (DEPTH, 2 * FFN_HIDDEN), 0.01),
        'f_w_down': nrm(ks[21], (DEPTH, FFN_HIDDEN, d), FFN_HIDDEN ** -0.5),
        'final_g': gain(ks[22], (d,)),
    }


def reference(x, norm_mix_g, norm_ffn_g, a_w_in, a_g_v, a_w_s, a_b_s, a_w_out,
              b_w_in, b_a_re, b_a_im, b_log_dt, b_b_re, b_b_im, b_c_re, b_c_im, b_d, b_w_glu,
              f_w_up, f_conv_w, f_conv_b, f_w_down, final_g):
    h = x
    for i in range(DEPTH):
        j = i // N_MIXERS
        hn = rms_norm(h, norm_mix_g[i])
        if i % N_MIXERS == 0:
            h = h + chunked_sgu_mixer(hn, a_w_in[j], a_g_v[j], a_w_s[j], a_b_s[j], a_w_out[j])
        else:
            h = h + s5_mixer(hn, b_w_in[j], b_a_re[j], b_a_im[j], b_log_dt[j],
                             b_b_re[j], b_b_im[j], b_c_re[j], b_c_im[j], b_d[j], b_w_glu[j])
        h = h + conv_glu_ffn(rms_norm(h, norm_ffn_g[i]), f_w_up[i], f_conv_w[i],
                             f_conv_b[i], f_w_down[i])
    return rms_norm(h, final_g)
```

```python
import math
from contextlib import ExitStack
import numpy as np
import concourse.bass as bass
import concourse.mybir as mybir
from concourse.bass_utils import run_bass_kernel_spmd

F32 = mybir.dt.float32
BF16 = mybir.dt.bfloat16
AF = mybir.ActivationFunctionType
ALU = mybir.AluOpType

D = 2048
KC = 16
TT = 512
SEQ = 4096
FH = 5632
HC = 44
DEPTH = 4
EPS = 1e-6
NSLOT = 6
DBG = {}
SCAN_ENG2 = "pool"
ATTACH_WAITS = True
PI = math.pi


class Buf:
    __slots__ = ("name", "w", "r")

    def __init__(self, name):
        self.name = name
        self.w = None
        self.r = {}


class DSem:
    def __init__(self, nc, name):
        self.h = nc.alloc_semaphore(name)
        self.count = 0


class Op:
    __slots__ = ("eng", "fn", "edeps", "ddeps", "sig", "tok", "dsem", "idx")

    def __init__(self, eng, fn, dsem, idx):
        self.eng = eng
        self.fn = fn
        self.dsem = dsem
        self.idx = idx
        self.sig = False
        self.tok = None
        self.edeps = {}
        self.ddeps = {}


class Prog:
    ENG = ("pe", "act", "dve", "pool", "sp")

    def __init__(self, nc):
        self.nc = nc
        self.ops = []
        self.esem = {e: nc.alloc_semaphore("es_" + e) for e in self.ENG}
        self.last = {e: None for e in self.ENG}
        self.dsems = []
        self.eobj = {"pe": nc.tensor, "act": nc.scalar, "dve": nc.vector, "pool": nc.gpsimd, "sp": nc.sync}
        self._bank = 0

    def dsem(self, name):
        s = DSem(self.nc, name)
        self.dsems.append(s)
        return s

    def bank(self):
        b = self._bank
        self._bank = (b + 1) % 8
        return b

    def op(self, eng, fn, reads=(), writes=(), dsem=None, extra=()):
        o = Op(eng, fn, dsem, len(self.ops))
        isdma = dsem is not None

        def add(d, raw):
            if d.dsem is not None:
                cur = o.ddeps.get(d.dsem)
                v = d.dsem.count
                if cur is None or cur < v:
                    o.ddeps[d.dsem] = v
                return
            if d.eng == eng and not isdma:
                if eng == "pe" or not raw:
                    return
            cur = o.edeps.get(d.eng)
            if cur is None or cur.idx < d.idx:
                o.edeps[d.eng] = d

        for b in reads:
            if b.w is not None:
                add(b.w, True)
        for b in writes:
            if b.w is not None:
                add(b.w, False)
            for d in b.r.values():
                add(d, False)
        for d in extra:
            add(d, True)
        for d in o.edeps.values():
            d.sig = True
        if isdma:
            dsem.count += 16
        key = ("d", id(dsem)) if isdma else eng
        for b in writes:
            b.w = o
            b.r = {}
        for b in reads:
            if b.w is not o:
                b.r[key] = o
        self.ops.append(o)
        if not isdma and fn is not None:
            self.last[eng] = o
        return o

    def barrier(self):
        lasts = [o for o in self.last.values() if o is not None]
        for e in self.ENG:
            o = Op(e, None, None, len(self.ops))
            for d in lasts:
                if d.eng != e:
                    o.edeps[d.eng] = d
                    d.sig = True
            for s in self.dsems:
                if s.count:
                    o.ddeps[s] = s.count
            self.ops.append(o)

    def emit(self):
        seen = {e: {} for e in self.ENG}
        cnt = {e: 0 for e in self.ENG}
        nw = 0
        for o in self.ops:
            E = self.eobj[o.eng]
            sn = seen[o.eng]
            waits = []
            for d in o.edeps.values():
                sem = self.esem[d.eng]
                if sn.get(id(sem), 0) < d.tok:
                    waits.append((sem, d.tok))
                    sn[id(sem)] = d.tok
            for s, v in o.ddeps.items():
                if sn.get(id(s.h), 0) < v:
                    waits.append((s.h, v))
                    sn[id(s.h)] = v
            att = None
            if ATTACH_WAITS and o.fn is not None and waits and o.dsem is None:
                att = waits.pop()
            for sem, v in waits:
                E.wait_ge(sem, v)
                nw += 1
            if o.fn is None:
                continue
            ins = o.fn(E)
            if att is not None:
                ins._wait_ge(att[0], att[1])
            if o.dsem is not None:
                ins.then_inc(o.dsem.h, 16)
            elif o.sig:
                cnt[o.eng] += 1
                o.tok = cnt[o.eng]
                ins.then_inc(self.esem[o.eng], 1)
        return nw

    def mm(self, out, lhsT, rhs, start, stop, reads, writes, tp=None):
        if tp is None:
            fn = lambda E: E.matmul(out, lhsT, rhs, start=start, stop=stop)
        else:
            fn = lambda E: E.matmul(out, lhsT, rhs, start=start, stop=stop, tile_position=tp)
        return self.op("pe", fn, reads, writes)

    def act(self, out, in_, func, reads, writes, bias=None, scale=None, accum_out=None):
        kw = {}
        if bias is not None:
            kw["bias"] = bias
        if scale is not None:
            kw["scale"] = scale
        if accum_out is not None:
            kw["accum_out"] = accum_out
        return self.op("act", lambda E: E.activation(out=out, in_=in_, func=func, **kw), reads, writes)

    def tt(self, eng, out, in0, in1, op, reads, writes):
        return self.op(eng, lambda E: E.tensor_tensor(out=out, in0=in0, in1=in1, op=op), reads, writes)

    def stt(self, out, in0, scalar, in1, op0, op1, reads, writes):
        return self.op("dve", lambda E: E.scalar_tensor_tensor(out=out, in0=in0, scalar=scalar, in1=in1,
                                                               op0=op0, op1=op1), reads, writes)

    def ts(self, eng, out, in0, s1, s2, op0, op1, reads, writes):
        if s2 is None:
            fn = lambda E: E.tensor_scalar(out=out, in0=in0, scalar1=s1, scalar2=None, op0=op0)
        else:
            fn = lambda E: E.tensor_scalar(out=out, in0=in0, scalar1=s1, scalar2=s2, op0=op0, op1=op1)
        return self.op(eng, fn, reads, writes)

    def copy(self, eng, out, in_, reads, writes):
        if eng == "act":
            return self.act(out, in_, AF.Copy, reads, writes)
        return self.op(eng, lambda E: E.tensor_copy(out=out, in_=in_), reads, writes)

    def memset(self, eng, ap, val, writes):
        return self.op(eng, lambda E: E.memset(ap, val), (), writes)

    def recip(self, out, in_, reads, writes):
        return self.op("dve", lambda E: E.reciprocal(out=out, in_=in_), reads, writes)

    def dma(self, q, out, in_, dsem, reads, writes, cast=False):
        if cast:
            fn = lambda E: E.dma_start(out=out, in_=in_, max_dma_last_dim=8192)
        else:
            fn = lambda E: E.dma_start(out=out, in_=in_)
        return self.op(q, fn, reads, writes, dsem=dsem)


class Stream:
    def __init__(self, P, nc, sched):
        self.P = P
        self.sched = sched
        self.slots = [nc.alloc_sbuf_tensor(f"ws{i}", [128, 4096], BF16) for i in range(NSLOT)]
        self.bufs = [Buf(f"ws{i}") for i in range(NSLOT)]
        self.sems = [P.dsem(f"wsem{i}") for i in range(NSLOT)]
        self.nrec = 0
        self.cur = 0
        self.rel = 0

    def _rec(self, i):
        kind, src, rd = self.sched[i]
        s = i % NSLOT
        if kind == "w":
            n = src.shape[1]
            dst = self.slots[s][:].rearrange("p (k c) -> p k c", c=256)[:, 0:n, :]
            self.P.dma("pool", dst, src, self.sems[s], rd, [self.bufs[s]], cast=True)
        else:
            n = src.shape[1]
            dst = self.slots[s][:, 0:n]
            self.P.dma("sp", dst, src, self.sems[s], rd, [self.bufs[s]])

    def _fill(self):
        hi = min(self.rel + NSLOT - 1, len(self.sched) - 1)
        while self.nrec <= hi:
            self._rec(self.nrec)
            self.nrec += 1

    def get(self, kind):
        c = self.cur
        assert self.sched[c][0] == kind, (c, self.sched[c][0], kind)
        self._fill()
        assert c < self.nrec, "too many outstanding stream tiles"
        self.cur += 1
        s = c % NSLOT
        return self.slots[s], self.bufs[s]

    def release(self, n=1):
        self.rel += n
        assert self.rel <= self.cur
        self._fill()


def build(layers, ntok, final, parts=None):
    parts = parts or {}
    def part(l):
        dm, df = parts.get(l, (True, True))
        return (dm and not DBG.get("no_mix")), (df and not DBG.get("no_ffn"))
    nc = bass.Bass("TRN2", target_bir_lowering=False)
    P = Prog(nc)
    ntiles = ntok // TT

    in_names = []

    def din(name, shape, dt=F32):
        in_names.append(name)
        return nc.dram_tensor(name, list(shape), dt, kind="ExternalInput").ap()

    xT = din("xT", [D, ntok])
    yT = nc.dram_tensor("yT", [D, ntok], F32, kind="ExternalOutput").ap()
    cst = din("cst", [128, 3, 128])
    Ls = {}
    for l in layers:
        j = l // 2
        d = {}
        d["nmg"] = din(f"nmg{l}", [128, 16])
        d["nfg"] = din(f"nfg{l}", [128, 16])
        d["cvw"] = din(f"cvw{l}", [128, 88, 4])
        if part(l)[1]:
            d["wup"] = din(f"wup{l}", [44, 128, 16, 256])
            d["wdn"] = din(f"wdn{l}", [8, 128, 44, 256])
        if not part(l)[0]:
            pass
        elif l % 2 == 0:
            d["awin"] = din(f"awin{l}", [16, 128, 16, 256])
            d["awout"] = din(f"awout{l}", [8, 128, 16, 256])
            d["agv"] = din(f"agv{l}", [128, 16])
            d["awsT"] = din(f"awsT{l}", [128, 16, 128])
            d["abs"] = din(f"abs{l}", [128, 16, 128])
        else:
            d["bwin"] = din(f"bwin{l}", [8, 128, 16, 256])
            d["bwglu"] = din(f"bwglu{l}", [16, 128, 16, 256])
            d["bd"] = din(f"bd{l}", [128, 16])
            d["s5a_p"] = din(f"s5a_p{l}", [64, 3, 128])
            d["s5b_p"] = din(f"s5b_p{l}", [64, 2, 128, 16])
            d["s5c_p"] = din(f"s5c_p{l}", [64, 2, 128, 16])
            d["s5a_r"] = din(f"s5a_r{l}", [128, 3, 16, 64])
            d["s5b_r"] = din(f"s5b_r{l}", [128, 16, 2, 2, 64])
            d["s5a_c"] = din(f"s5a_c{l}", [128, 3, 64])
            d["s5c_c"] = din(f"s5c_c{l}", [128, 64, 2, 32])
            d["bs_scr"] = nc.dram_tensor(f"bs_scr{l}", [8, 128, 4096], BF16, kind="Internal").ap()
            d["kc_scr"] = nc.dram_tensor(f"kc_scr{l}", [16, 128, 3072], BF16, kind="Internal").ap()
            d["scrB"] = Buf(f"scr{l}")
        Ls[l] = d
    Alay = [l for l in layers if l % 2 == 0 and part(l)[0]]
    Blay = [l for l in layers if l % 2 == 1 and part(l)[0]]

    csem = P.dsem("csem")
    cB = Buf("consts")

    def cload(name, shape, src, dt=F32):
        t = nc.alloc_sbuf_tensor(name, list(shape), dt)
        P.dma("sp", t[:], src, csem, [], [cB])
        return t

    cst_t = cload("cst_t", [128, 3, 128], cst)
    ident = cst_t[:, 0, :]
    bdmask = cst_t[:, 1, :]
    cmask = cst_t[:, 2, :]
    ones_bf = nc.alloc_sbuf_tensor("ones_bf", [128, 128], BF16)
    P.memset("dve", ones_bf[:], 1.0, [cB])
    eps_t = nc.alloc_sbuf_tensor("eps_t", [128, 1], F32)
    P.memset("dve", eps_t[:], EPS, [cB])
    for l in layers:
        d = Ls[l]
        d["nmg_t"] = cload(f"nmg_t{l}", [128, 16], d["nmg"])
        d["nfg_t"] = cload(f"nfg_t{l}", [128, 16], d["nfg"])
        d["cvw_t"] = cload(f"cvw_t{l}", [128, 88, 4], d["cvw"])
        d["zc"] = nc.alloc_sbuf_tensor(f"zc{l}", [128, 88, 2], F32)
        d["zcB"] = Buf(f"zc{l}")
        P.memset("dve", d["zc"][:], 0.0, [d["zcB"]])
        d["E"] = nc.alloc_sbuf_tensor(f"E{l}", [128, 88, 2], F32)
        d["EB"] = Buf(f"E{l}")
        if not part(l)[0]:
            pass
        elif l % 2 == 0:
            d["agv_t"] = cload(f"agv_t{l}", [128, 16], d["agv"])
            d["wsT"] = nc.alloc_sbuf_tensor(f"wsT{l}", [128, 16, 128], BF16)
            d["wsTB"] = Buf(f"wsT{l}")
        else:
            d["scar"] = nc.alloc_sbuf_tensor(f"scar{l}", [128, 64, 2], F32)
            d["scarB"] = Buf(f"scar{l}")
            P.memset("dve", d["scar"][:], 0.0, [d["scarB"]])
            d["A2a"] = nc.alloc_sbuf_tensor(f"A2a{l}", [128, 64, 2], F32)
            d["A2b"] = nc.alloc_sbuf_tensor(f"A2b{l}", [128, 64, 2], F32)
            d["A2B"] = Buf(f"A2{l}")
    final_g = None
    if final:
        fg = din("fing", [128, 16])
        final_g = cload("fing_t", [128, 16], fg)

    scr_sem = P.dsem("scrsem")

    def red_sin(pfx, NP, shape, arg, out, pB, tmp, tmp2):
        P.copy("dve", tmp2, arg, [pB], [pB])
        for m in range(1, 6):
            P.ts("dve", tmp, arg, (2 * m - 1) * PI, -2.0 * PI, ALU.is_gt, ALU.mult, [pB], [pB])
            P.tt("dve", tmp2, tmp2, tmp, ALU.add, [pB], [pB])
        P.ts("dve", tmp2, tmp2, -PI, PI, ALU.max, ALU.min, [pB], [pB])
        P.act(out, tmp2, AF.Sin, [pB], [pB])

    def coeffs(es, pfx, NP, fshape, a_t, npow, pB):
        F = int(np.prod(fshape))
        mk = lambda n, s=None: es.enter_context(nc.sbuf_tensor(pfx + n, [NP] + list(s or [F]), F32))
        are = a_t[0:NP, 0].rearrange("p a b -> p (a b)") if len(fshape) == 2 else a_t[0:NP, 0]
        aim = a_t[0:NP, 1].rearrange("p a b -> p (a b)") if len(fshape) == 2 else a_t[0:NP, 1]
        ldt = a_t[0:NP, 2].rearrange("p a b -> p (a b)") if len(fshape) == 2 else a_t[0:NP, 2]
        dt_, mag, th, t1, t2, cs, abr, abi = [mk(n) for n in
                                             ("dt", "mag", "th", "t1", "t2", "cs", "abr", "abi")]
        sn = dt_
        Pr = mk("Pr", [npow + 1, F])
        Pi_ = mk("Pi", [npow + 1, F])
        qr, qi = mk("qr"), mk("qi")
        R = [pB]
        P.act(dt_[:], ldt, AF.Exp, [cB, pB], R)
        P.tt("dve", t1[:], dt_[:], are, ALU.mult, R, R)
        P.act(mag[:], t1[:], AF.Exp, R, R)
        P.tt("dve", th[:], dt_[:], aim, ALU.mult, R, R)
        red_sin(pfx, NP, [NP, F], th[:], sn[:], pB, t1[:], t2[:])
        P.ts("dve", th[:], th[:], PI / 2, None, ALU.add, None, R, R)
        red_sin(pfx, NP, [NP, F], th[:], cs[:], pB, t1[:], t2[:])
        P.tt("dve", abr[:], mag[:], cs[:], ALU.mult, R, R)
        P.tt("dve", abi[:], mag[:], sn[:], ALU.mult, R, R)
        P.memset("dve", Pr[:, 0, :], 1.0, R)
        P.memset("dve", Pi_[:, 0, :], 0.0, R)
        for k in range(npow):
            P.tt("dve", t1[:], Pr[:, k, :], abr[:], ALU.mult, R, R)
            P.tt("dve", t2[:], Pi_[:, k, :], abi[:], ALU.mult, R, R)
            P.tt("dve", Pr[:, k + 1, :], t1[:], t2[:], ALU.subtract, R, R)
            P.tt("dve", t1[:], Pr[:, k, :], abi[:], ALU.mult, R, R)
            P.tt("dve", t2[:], Pi_[:, k, :], abr[:], ALU.mult, R, R)
            P.tt("dve", Pi_[:, k + 1, :], t1[:], t2[:], ALU.add, R, R)
        den = th
        P.tt("dve", den[:], are, are, ALU.mult, R, R)
        P.tt("dve", t1[:], aim, aim, ALU.mult, R, R)
        P.tt("dve", den[:], den[:], t1[:], ALU.add, R, R)
        P.recip(den[:], den[:], R, R)
        am1 = mag
        P.ts("dve", am1[:], abr[:], -1.0, None, ALU.add, None, R, R)
        P.tt("dve", t1[:], am1[:], are, ALU.mult, R, R)
        P.tt("dve", t2[:], abi[:], aim, ALU.mult, R, R)
        P.tt("dve", t1[:], t1[:], t2[:], ALU.add, R, R)
        P.tt("dve", qr[:], t1[:], den[:], ALU.mult, R, R)
        P.tt("dve", t1[:], abi[:], are, ALU.mult, R, R)
        P.tt("dve", t2[:], am1[:], aim, ALU.mult, R, R)
        P.tt("dve", t1[:], t1[:], t2[:], ALU.subtract, R, R)
        P.tt("dve", qi[:], t1[:], den[:], ALU.mult, R, R)
        return Pr, Pi_, qr, qi

    def cgroup(eng, pB, t1, t2, tB, ar, ai, br, bi, o_re, o_im, neg_im, oB):
        rd = [pB, tB]
        P.tt(eng, t1, ar, br, ALU.mult, rd, [tB])
        P.tt(eng, t2, ai, bi, ALU.mult, rd, [tB])
        P.tt(eng, o_re, t1, t2, ALU.subtract, rd, [oB])
        P.tt(eng, t1, ar, bi, ALU.mult, rd, [tB])
        P.tt(eng, t2, ai, br, ALU.mult, rd, [tB])
        if not neg_im:
            P.tt(eng, o_im, t1, t2, ALU.add, rd, [oB])
        elif eng == "dve":
            P.stt(o_im, t1, -1.0, t2, ALU.mult, ALU.subtract, rd, [oB])
        else:
            P.tt(eng, t1, t1, t2, ALU.add, rd, [tB])
            P.ts(eng, o_im, t1, -1.0, None, ALU.mult, None, rd, [oB])

    def prep_A(l):
        d = Ls[l]
        pB = Buf(f"prepA{l}")
        P.barrier()
        with nc.sbuf_tensor(f"wsraw{l}", [128, 16, 128], F32) as raw:
            P.dma("sp", raw[:], d["awsT"], scr_sem, [], [pB])
            P.tt("dve", d["wsT"][:], raw[:], cmask.unsqueeze(1).to_broadcast([128, 16, 128]), ALU.mult,
                 [pB, cB], [d["wsTB"]])

    def prep_B(l):
        d = Ls[l]
        pB = Buf(f"prepB{l}")
        R = [pB]
        bs_scr, kc_scr, scrB = d["bs_scr"], d["kc_scr"], d["scrB"]
        P.barrier()
        with ExitStack() as es:
            sa = lambda n, sh, dt=F32: es.enter_context(nc.sbuf_tensor(n + str(l), sh, dt))
            a_t = sa("l3a", [128, 3, 64])
            c_t = sa("l3c", [128, 64, 2, 32])
            o0 = sa("l3o0", [128, 16, 8, 2, 32], BF16)
            o1 = sa("l3o1", [128, 16, 8, 2, 32], BF16)
            t1 = sa("l3t1", [128, 16, 32])
            t2 = sa("l3t2", [128, 16, 32])
            P.dma("sp", a_t[:], d["s5a_c"], scr_sem, [], R)
            P.dma("sp", c_t[:], d["s5c_c"], scr_sem, [], R)
            Pr, Pi_, qr, qi = coeffs(es, f"l3_{l}", 128, [64], a_t, 8, pB)
            for c in range(2):
                P.copy("dve", d["A2a"][:, :, c], Pr[:, 8, :], R, [d["A2B"]])
            P.ts("dve", d["A2b"][:, :, 0], Pi_[:, 8, :], -1.0, None, ALU.mult, None, R, [d["A2B"]])
            P.copy("dve", d["A2b"][:, :, 1], Pi_[:, 8, :], R, [d["A2B"]])
            t3 = sa("l3t3", [128, 16, 32])
            t4 = sa("l3t4", [128, 16, 32])
            tmps = {"dve": (t1, t2, Buf("l3tA")), "pool": (t3, t4, Buf("l3tB"))}
            obufs = [{"dve": Buf("l3o0d"), "pool": Buf("l3o0p")}, {"dve": Buf("l3o1d"), "pool": Buf("l3o1p")}]
            for pg in range(4):
                o = (o0, o1)[pg % 2]
                oB = obufs[pg % 2]
                cr = c_t[:, pg * 16:(pg + 1) * 16, 0, :]
                ci = c_t[:, pg * 16:(pg + 1) * 16, 1, :]
                for r in range(8):
                    eng = "pool" if r % 4 == 3 else "dve"
                    ta, tb_, tB = tmps[eng]
                    prb = Pr[:, r + 1, pg * 16:(pg + 1) * 16].unsqueeze(2).to_broadcast([128, 16, 32])
                    pib = Pi_[:, r + 1, pg * 16:(pg + 1) * 16].unsqueeze(2).to_broadcast([128, 16, 32])
                    cgroup(eng, pB, ta[:], tb_[:], tB, cr, ci, prb, pib, o[:, :, r, 0, :], o[:, :, r, 1, :],
                           True, oB[eng])
                dst = kc_scr[4 * pg:4 * pg + 4, :, 1024:3072].rearrange("f p (k x) -> p f k x", k=4)
                src = o[:].rearrange("p (f k) r e c -> p f k (r e c)", k=4)
                P.dma("sp", dst, src, scr_sem, [oB["dve"], oB["pool"]], [scrB])
        P.barrier()
        with ExitStack() as es:
            sa = lambda n, sh, dt=F32: es.enter_context(nc.sbuf_tensor(n + str(l), sh, dt))
            a_t = sa("l2a", [128, 3, 16, 64])
            b_t = sa("l2b", [128, 16, 2, 2, 64])
            bb = sa("l2bb", [128, 2, 16, 2, 64])
            o0 = sa("l2o0", [128, 2, 8, 2, 128], BF16)
            o1 = o0
            t1 = sa("l2t1", [128, 16, 2, 64])
            t2 = sa("l2t2", [128, 16, 2, 64])
            P.dma("sp", a_t[:], d["s5a_r"], scr_sem, [], R)
            P.dma("sp", b_t[:], d["s5b_r"], scr_sem, [], R)
            Pr, Pi_, qr, qi = coeffs(es, f"l2_{l}", 128, [16, 64], a_t, 7, pB)
            bc = lambda t: t.rearrange("p (f x) -> p f x", x=64).unsqueeze(2).to_broadcast([128, 16, 2, 64])
            br, bi = b_t[:, :, 0], b_t[:, :, 1]
            P.tt("dve", t1[:], br, bc(qr[:]), ALU.mult, R, R)
            P.tt("dve", t2[:], bi, bc(qi[:]), ALU.mult, R, R)
            P.tt("dve", bb[:, 0], t1[:], t2[:], ALU.subtract, R, R)
            P.tt("dve", t1[:], bi, bc(qr[:]), ALU.mult, R, R)
            P.tt("dve", t2[:], br, bc(qi[:]), ALU.mult, R, R)
            P.tt("dve", bb[:, 1], t1[:], t2[:], ALU.add, R, R)
            t3 = sa("l2t3", [128, 2, 2, 64])
            t4 = sa("l2t4", [128, 2, 2, 64])
            tmps = {"dve": (t1[:, 0:2], t2[:, 0:2], Buf("l2tA")), "pool": (t3[:], t4[:], Buf("l2tB"))}
            oB = {"dve": Buf("l2od"), "pool": Buf("l2op")}
            for f2 in range(8):
                o = o0
                fs = slice(2 * f2, 2 * f2 + 2)
                ov = o[:].rearrange("p f q e (m x) -> p f q e m x", m=2)
                for q in range(8):
                    eng = "pool" if q % 4 == 3 else "dve"
                    ta, tb_, tB = tmps[eng]
                    k = 7 - q
                    pr = bc(Pr[:, k, :])[:, fs]
                    pi = bc(Pi_[:, k, :])[:, fs]
                    cgroup(eng, pB, ta, tb_, tB, bb[:, 0, fs], bb[:, 1, fs], pr, pi, ov[:, :, q, 0], ov[:, :, q, 1],
                           False, oB[eng])
                P.dma("sp", bs_scr[f2], o[:].rearrange("p f q e x -> p (f q e x)"), scr_sem,
                      [oB["dve"], oB["pool"]], [scrB])
        P.barrier()
        with ExitStack() as es:
            sa = lambda n, sh, dt=F32: es.enter_context(nc.sbuf_tensor(n + str(l), sh, dt))
            a_t = sa("l1a", [64, 3, 128])
            b_t = sa("l1b", [64, 2, 128, 16])
            c_t = sa("l1c", [64, 2, 128, 16])
            bb = sa("l1bb", [64, 2, 128, 16])
            cp = sa("l1cp", [64, 2, 8, 4, 128])
            t1 = sa("l1t1", [64, 128, 16])
            t2 = sa("l1t2", [64, 128, 16])
            kt = sa("l1k", [128, 8, 128])
            o0 = sa("l1o0", [128, 8, 128], BF16)
            o1 = sa("l1o1", [128, 8, 128], BF16)
            P.dma("sp", a_t[:], d["s5a_p"], scr_sem, [], R)
            P.dma("sp", b_t[:], d["s5b_p"], scr_sem, [], R)
            P.dma("sp", c_t[:], d["s5c_p"], scr_sem, [], R)
            Pr, Pi_, qr, qi = coeffs(es, f"l1_{l}", 64, [128], a_t, 7, pB)
            bc = lambda t, n=128: t.unsqueeze(2).to_broadcast([64, n, 16])
            br, bi = b_t[:, 0], b_t[:, 1]
            P.tt("dve", t1[:], br, bc(qr[:]), ALU.mult, R, R)
            P.tt("dve", t2[:], bi, bc(qi[:]), ALU.mult, R, R)
            P.tt("dve", bb[:, 0], t1[:], t2[:], ALU.subtract, R, R)
            P.tt("dve", t1[:], bi, bc(qr[:]), ALU.mult, R, R)
            P.tt("dve", t2[:], br, bc(qi[:]), ALU.mult, R, R)
            P.tt("dve", bb[:, 1], t1[:], t2[:], ALU.add, R, R)
            obufs = [Buf("l1o0"), Buf("l1o1")]
            ktB = Buf("l1kt")
            t3 = sa("l1t3", [64, 32, 16])
            t4 = sa("l1t4", [64, 32, 16])
            tmps = {"dve": (t1[:, 0:32], t2[:, 0:32], Buf("l1tA")), "pool": (t3[:], t4[:], Buf("l1tB"))}
            cpB = {"dve": Buf("l1cpd"), "pool": Buf("l1cpp")}
            CPR = [pB, cpB["dve"], cpB["pool"]]
            for fg in range(4):
                gs = slice(32 * fg, 32 * fg + 32)
                cr, ci = c_t[:, 0, gs], c_t[:, 1, gs]
                for tau in range(8):
                    eng = "pool" if tau % 4 == 3 else "dve"
                    ta, tb_, tB = tmps[eng]
                    prb = bc(Pr[:, tau, gs], 32)
                    pib = bc(Pi_[:, tau, gs], 32)
                    cpr = cp[:, 0, tau].rearrange("p f (g c) -> p (f g) c", c=16)
                    cpi = cp[:, 1, tau].rearrange("p f (g c) -> p (f g) c", c=16)
                    cgroup(eng, pB, ta, tb_, tB, cr, ci, prb, pib, cpr, cpi, True, cpB[eng])
                for fl in range(4):
                    fc = 4 * fg + fl
                    o = (o0, o1)[fc % 2]
                    oB = obufs[fc % 2]
                    lre = bb[:, 0, 8 * fc:8 * fc + 8, :].rearrange("p g c -> p (g c)")
                    lim = bb[:, 1, 8 * fc:8 * fc + 8, :].rearrange("p g c -> p (g c)")
                    for hf in range(2):
                        bk = P.bank()
                        rre = cp[:, 0, 4 * hf:4 * hf + 4, fl, :]
                        rim = cp[:, 1, 4 * hf:4 * hf + 4, fl, :]
                        pso = ps[bk][:].rearrange("p (t x) -> p t x", x=128)
                        P.mm(pso, lre, rre, True, False, CPR, [psB[bk]])
                        P.mm(pso, lim, rim, False, True, CPR, [psB[bk]])
                        P.tt("dve", kt[:, 4 * hf:4 * hf + 4, :], ps[bk][:].rearrange("p (t x) -> p t x", x=128),
                             bdmask.unsqueeze(1).to_broadcast([128, 4, 128]), ALU.mult, [psB[bk], cB], [ktB])
                    P.stt(kt[:, 0, :], ident, d["bd_t"][:, fc:fc + 1], kt[:, 0, :], ALU.mult, ALU.add,
                          [ktB, cB], [ktB])
                    P.copy("dve", o[:], kt[:], [ktB], [oB])
                    P.dma("sp", kc_scr[fc, :, 0:1024], o[:].rearrange("p t x -> p (t x)"), scr_sem, [oB], [scrB])

    ps = [nc.alloc_psum_tensor(f"ps{i}", [128, 512], F32) for i in range(8)]
    psB = [Buf(f"ps{i}") for i in range(8)]
    for l in Blay:
        Ls[l]["bd_t"] = cload(f"bd_t{l}", [128, 16], Ls[l]["bd"])
    for l in Alay:
        if part(l)[0]:
            prep_A(l)
    for l in Blay:
        if part(l)[0]:
            prep_B(l)
    P.barrier()

    h = nc.alloc_sbuf_tensor("h", [128, 16, TT], F32)
    hB = [Buf(f"h{k}") for k in range(16)]
    hn = nc.alloc_sbuf_tensor("hn", [128, 16, TT], BF16)
    hnB = [Buf(f"hn{k}") for k in range(16)]
    U = nc.alloc_sbuf_tensor("U", [128, 16384], F32)
    RB = [Buf(f"U{i}") for i in range(64)]
    a_bf = U[:, 0:11264].bitcast(BF16).rearrange("p (c t) -> p c t", t=TT)
    u_bf = U[:, 0:4096].bitcast(BF16).rearrange("p (c t) -> p c t", t=TT)
    v_bf = U[:, 4096:8192].bitcast(BF16).rearrange("p (b f) -> p b f", f=2048)
    vn_bf = U[:, 8192:12288].bitcast(BF16).rearrange("p (b f) -> p b f", f=2048)
    XS = U[:, 4096:12288].rearrange("p (a r j) -> p a r j", r=2, j=64)
    Sb = U[:, 12288:16384].bitcast(BF16).rearrange("p (a r j) -> p a r j", r=2, j=64)
    XSB = RB[16:48]
    SbB = RB[48:64]
    ostage = U[:, 0:8192].rearrange("p (c t) -> p c t", t=TT)
    NACC = 4
    accs = [nc.alloc_sbuf_tensor(f"acc{i}", [128, TT], F32) for i in range(NACC)]
    accB = [Buf(f"acc{i}") for i in range(NACC)]
    acc_i = [0]

    def acc():
        i = acc_i[0]
        acc_i[0] = (i + 1) % NACC
        return accs[i], accB[i]

    small = nc.alloc_sbuf_tensor("small", [128, 16], F32)
    smallB = Buf("small")
    stmp = nc.alloc_sbuf_tensor("stmp", [128, 4, 64, 2], F32)
    stB = [Buf("stA0"), Buf("stB0"), Buf("stA1"), Buf("stB1")]
    abb = nc.alloc_sbuf_tensor("abb", [128, 16, 128], F32)
    abbB = Buf("abb")
    abb_sem = P.dsem("abbsem")
    xsem = P.dsem("xsem")
    osem = P.dsem("osem")

    sched = []
    for t in range(ntiles):
        for l in layers:
            d = Ls[l]
            if not part(l)[0]:
                pass
            elif l % 2 == 0:
                sched += [("w", d["awin"][i], []) for i in range(16)]
                sched += [("w", d["awout"][i], []) for i in range(8)]
            else:
                sched += [("w", d["bwin"][i], []) for i in range(8)]
                sched += [("t", d["bs_scr"][i], [d["scrB"]]) for i in range(8)]
                sched += [("t", d["kc_scr"][i], [d["scrB"]]) for i in range(16)]
                sched += [("w", d["bwglu"][i], []) for i in range(16)]
            if not part(l)[1]:
                continue
            sched += [("w", d["wup"][i], []) for i in range(44)]
            for mg in range(8):
                for (h0, n) in ((0, 16), (16, 16), (32, 12)):
                    sched.append(("w", d["wdn"][mg, :, h0:h0 + n, :], []))
    WS = Stream(P, nc, sched)

    def wview(slot):
        return slot[:].rearrange("p (k c) -> p k c", c=256)

    dsem_dbg = P.dsem("dbgsem")
    cur_tile = [0]

    def dump(name, ap, bufs):
        if DBG.get("dump_tile") != cur_tile[0] or name not in DBG.get("dump", ()):
            return
        t_ = nc.dram_tensor("dbg_" + name, list(ap.shape), ap.dtype, kind="ExternalOutput").ap()
        P.dma("sp", t_, ap, dsem_dbg, list(bufs), [])

    def emit_norm(g_t, gB, to_h=False):
        bk = P.bank()
        for k in range(16):
            if k % 2 == 0:
                P.act(hn[:, k, :], h[:, k, :], AF.Square, [hB[k]], [hnB[k]])
            else:
                P.tt("dve", hn[:, k, :], h[:, k, :], h[:, k, :], ALU.mult, [hB[k]], [hnB[k]])
            P.mm(ps[bk][:], ones_bf[:], hn[:, k, :], k == 0, k == 15, [hnB[k], cB], [psB[bk]])
        rs, rsB = acc()
        P.act(rs[:], ps[bk][:], AF.Sqrt, [psB[bk], cB], [rsB], bias=eps_t[:], scale=1.0 / D)
        P.recip(rs[:], rs[:], [rsB], [rsB])
        dump("rs", rs[:], [rsB])
        for k in range(16):
            if to_h:
                P.stt(ostage[:, k, :], h[:, k, :], g_t[:, k:k + 1], rs[:], ALU.mult, ALU.mult, [hB[k], rsB, gB],
                      RB[2 * k:2 * k + 2])
            else:
                P.stt(hn[:, k, :], h[:, k, :], g_t[:, k:k + 1], rs[:], ALU.mult, ALU.mult, [hB[k], rsB, gB],
                      [hnB[k]])

    def proj_fm(nchunks_pairs, rd_act, rdB, evac):
        for t in range(nchunks_pairs):
            w, wB = WS.get("w")
            wv = wview(w)
            for c in range(2):
                m = 2 * t + c
                bk = P.bank()
                for k in range(16):
                    P.mm(ps[bk][:], wv[:, k, c * 128:(c + 1) * 128], rd_act[:, k, :], k == 0, k == 15,
                         [wB, rdB(k)], [psB[bk]])
                evac(m, bk)
            WS.release()

    def resid_add(m, bk):
        P.tt("dve", h[:, m, :], ps[bk][:], h[:, m, :], ALU.add, [psB[bk], hB[m]], [hB[m]])

    def emit_ffn(l):
        d = Ls[l]
        cv, zc, zcB, E, EB = d["cvw_t"], d["zc"], d["zcB"], d["E"], d["EB"]
        P.tt("dve", E[:, :, 0], zc[:, :, 1], cv[:, :, 1], ALU.mult, [zcB, cB], [EB])
        P.tt("dve", E[:, :, 1], zc[:, :, 0], cv[:, :, 0], ALU.mult, [zcB, cB], [EB])
        P.tt("dve", E[:, :, 0], E[:, :, 0], E[:, :, 1], ALU.add, [EB], [EB])
        P.tt("dve", E[:, :, 1], zc[:, :, 1], cv[:, :, 0], ALU.mult, [zcB, cB, EB], [EB])

        def conv(bk, ch, a_t, aB):
            p = ps[bk]
            P.act(a_t[:], p[:], AF.Identity, [psB[bk], cB], [aB], bias=cv[:, ch, 3:4], scale=cv[:, ch, 2:3])
            P.act(zc[:, ch, :], p[:, 510:512], AF.Copy, [psB[bk]], [zcB])
            P.stt(a_t[:, 1:512], p[:, 0:511], cv[:, ch, 1:2], a_t[:, 1:512], ALU.mult, ALU.add,
                  [psB[bk], aB, cB], [aB])
            P.stt(a_t[:, 2:512], p[:, 0:510], cv[:, ch, 0:1], a_t[:, 2:512], ALU.mult, ALU.add,
                  [psB[bk], aB, cB], [aB])
            P.tt("dve", a_t[:, 0:2], a_t[:, 0:2], E[:, ch, :], ALU.add, [aB, EB], [aB])

        for pg in range(22):
            wg, wgB = WS.get("w")
            wv_, wvB = WS.get("w")
            wgv, wvv = wview(wg), wview(wv_)
            for c in range(2):
                hc = 2 * pg + c
                bg = P.bank()
                bv = P.bank()
                for k in range(16):
                    P.mm(ps[bg][:], wgv[:, k, c * 128:(c + 1) * 128], hn[:, k, :], k == 0, k == 15,
                         [wgB, hnB[k]], [psB[bg]])
                for k in range(16):
                    P.mm(ps[bv][:], wvv[:, k, c * 128:(c + 1) * 128], hn[:, k, :], k == 0, k == 15,
                         [wvB, hnB[k]], [psB[bv]])
                ag, agB = acc()
                av, avB = acc()
                conv(bg, hc, ag, agB)
                conv(bv, 44 + hc, av, avB)
                P.act(ag[:], ag[:], AF.Silu, [agB], [agB])
                P.tt("dve", a_bf[:, hc, :], ag[:], av[:], ALU.mult, [agB, avB], [RB[hc]])
            WS.release(2)
        for mg in range(8):
            b0 = P.bank()
            b1 = P.bank()
            bks = (b0, b1)
            for (h0, n) in ((0, 16), (16, 16), (32, 12)):
                w, wB = WS.get("w")
                wv = wview(w)
                for i in range(n):
                    hc = h0 + i
                    for c in range(2):
                        P.mm(ps[bks[c]][:], wv[:, i, c * 128:(c + 1) * 128], a_bf[:, hc, :], hc == 0, hc == 43,
                             [wB, RB[hc]], [psB[bks[c]]])
                WS.release()
            for c in range(2):
                resid_add(2 * mg + c, bks[c])

    def emit_mixA(l):
        d = Ls[l]
        gv, wsT, wsTB = d["agv_t"], d["wsT"], d["wsTB"]
        P.dma("sp", abb[:], d["abs"], abb_sem, [], [abbB])

        def ev_u(m, bk):
            P.act(u_bf[:, m, :], ps[bk][:], AF.Gelu_apprx_tanh, [psB[bk]], [RB[m]])

        dump("hn", hn[:], hnB)
        proj_fm(8, hn, lambda k: hnB[k], ev_u)
        dump("u", u_bf, RB[0:16])
        for t in range(8):
            w, wB = WS.get("w")
            wv = wview(w)
            bks = (P.bank(), P.bank())
            for tb in range(4):
                bk = bks[tb // 2]
                o = ps[bk][:, (tb % 2) * 256:(tb % 2 + 1) * 256]
                for k in range(16):
                    P.mm(o, hn[:, k, tb * 128:(tb + 1) * 128], wv[:, k, :], k == 0, k == 15,
                         [wB, hnB[k]], [psB[bk]])
            for hf in range(2):
                bk = bks[hf]
                P.act(v_bf[:, 2 * hf:2 * hf + 2, t * 256:(t + 1) * 256],
                      ps[bk][:].rearrange("p (a c) -> p a c", a=2), AF.Gelu_apprx_tanh,
                      [psB[bk]], RB[16 + 8 * hf:16 + 8 * hf + 8])
            WS.release()
        dump("v", v_bf, RB[16:32])
        ssq = small[:, 0:4]
        rsv = small[:, 4:8]
        for tb in range(4):
            P.act(vn_bf[:, tb, :], v_bf[:, tb, :], AF.Square, RB[16 + 4 * tb:20 + 4 * tb],
                  RB[32 + 4 * tb:36 + 4 * tb] + [smallB], accum_out=small[:, tb:tb + 1])
        P.act(rsv, ssq, AF.Sqrt, [smallB, cB], [smallB], bias=eps_t[:], scale=1.0 / D)
        P.recip(rsv, rsv, [smallB], [smallB])
        for tb in range(4):
            P.ts("dve", vn_bf[:, tb, :], v_bf[:, tb, :], small[:, 4 + tb:5 + tb], None, ALU.mult, None,
                 RB[16 + 4 * tb:20 + 4 * tb] + [smallB], RB[32 + 4 * tb:36 + 4 * tb])
        dump("vn", vn_bf, RB[32:48])
        dump("small", small[:], [smallB])
        for hd in range(16):
            bk = P.bank()
            for tb in range(4):
                P.mm(ps[bk][:, tb * 128:(tb + 1) * 128], vn_bf[:, tb, hd * 128:(hd + 1) * 128], wsT[:, hd, :],
                     True, True, RB[32 + 4 * tb:36 + 4 * tb] + [wsTB], [psB[bk]])
            tmp, tmpB = acc()
            P.stt(tmp[:].rearrange("p (a t) -> p a t", a=4), ps[bk][:].rearrange("p (a t) -> p a t", a=4),
                  gv[:, hd:hd + 1], abb[:, hd, :].unsqueeze(1).to_broadcast([128, 4, 128]), ALU.mult, ALU.add,
                  [psB[bk], cB, abbB], [tmpB])
            P.tt("dve", u_bf[:, hd, :], tmp[:], u_bf[:, hd, :], ALU.mult, [tmpB, RB[hd]], [RB[hd]])
        dump("us", u_bf, RB[0:16])
        proj_fm(8, u_bf, lambda k: RB[k], resid_add)

    def emit_mixB(l):
        d = Ls[l]
        scar, scarB, A2a, A2b, A2B = d["scar"], d["scarB"], d["A2a"], d["A2b"], d["A2B"]

        def ev_u(m, bk):
            P.act(u_bf[:, m, :], ps[bk][:], AF.Copy, [psB[bk]], [RB[m]])

        proj_fm(8, hn, lambda k: hnB[k], ev_u)
        XSv = U[:, 4096:12288].rearrange("p (a k r j) -> p k a (r j)", k=4, r=2, j=64)
        bsv = None
        for fg in range(4):
            bx = [P.bank() for _ in range(4)]
            for fl in range(4):
                fc = 4 * fg + fl
                if fc % 2 == 0:
                    bs, bsB = WS.get("t")
                    bsv = bs[:].rearrange("p (f q e x) -> p f q e x", f=2, q=8, e=2)
                uq = u_bf[:, fc, :].rearrange("p (j q) -> p q j", q=8)
                for e in range(2):
                    for q in range(8):
                        for k in range(4):
                            P.mm(ps[bx[k]][:, fl * 128 + e * 64:fl * 128 + e * 64 + 64],
                                 bsv[32 * k:32 * k + 32, fc % 2, q, e, :], uq[32 * k:32 * k + 32, q, :],
                                 q == 0, q == 7, [bsB, RB[fc]], [psB[bx[k]]], tp=(32 * k, 0))
                if fc % 2 == 1:
                    WS.release()
            for k in range(4):
                P.copy("act", XSv[:, k, 4 * fg:4 * fg + 4, :],
                       ps[bx[k]][:].rearrange("p (a x) -> p a x", a=4), [psB[bx[k]]], XSB[8 * fg:8 * fg + 8])
        tA = stmp[:, 0, :, :]
        tB = stmp[:, 1, :, :]
        tAB, tBB = stB[0], stB[1]
        for j in range(64):
            if j == 0:
                sp_, spr, spB = scar[:], scar[:, :, ::-1], [scarB]
            else:
                sp_, spr, spB = XS[:, :, :, j - 1], XS[:, :, ::-1, j - 1], XSB
            cur = XS[:, :, :, j]
            P.tt("dve", tA, sp_, A2a[:], ALU.mult, spB + [A2B], [tAB])
            P.tt("dve", tB, spr, A2b[:], ALU.mult, spB + [A2B], [tBB])
            P.tt("dve", cur, cur, tA, ALU.add, XSB + [tAB], XSB)
            P.tt("dve", cur, cur, tB, ALU.add, XSB + [tBB], XSB)
        P.copy("act", Sb[:, :, :, 1:64], XS[:, :, :, 0:63], XSB, SbB)
        P.copy("act", Sb[:, :, :, 0], scar[:], [scarB], SbB)
        P.copy("act", scar[:], XS[:, :, :, 63], XSB, [scarB])
        for fc in range(16):
            kc, kcB = WS.get("t")
            kd = kc[:, 0:1024].rearrange("p (t x) -> p t x", x=128)
            cs = kc[:, 1024:3072].rearrange("p (k r e c) -> p k r e c", k=4, r=8, e=2)
            bk = P.bank()
            yv = ps[bk][:].rearrange("p (j r) -> p j r", r=8)
            yr = ps[bk][:].rearrange("p (j r) -> p r j", r=8)
            uv = u_bf[:, fc, :].rearrange("p (j r) -> p j r", r=8)
            for tau in range(8):
                P.mm(yv[:, :, tau:8], kd[:, tau, :], uv[:, :, 0:8 - tau], tau == 0, False,
                     [kcB, RB[fc]], [psB[bk]])
            for r in range(8):
                for e in range(2):
                    for k in range(4):
                        pair = 4 * fc + k
                        P.mm(yr[32 * k:32 * k + 32, r, :], cs[:, k, r, e, :], Sb[:, pair, e, :], False,
                             (r == 7 and e == 1 and k == 3), [kcB] + SbB, [psB[bk]], tp=(0, 32 * k))
            WS.release()
            P.act(u_bf[:, fc, :], ps[bk][:], AF.Gelu_apprx_tanh, [psB[bk]], [RB[fc]])
        for mg in range(8):
            wa, waB = WS.get("w")
            wb, wbB = WS.get("w")
            wav, wbv = wview(wa), wview(wb)
            for c in range(2):
                m = 2 * mg + c
                ba = P.bank()
                bb_ = P.bank()
                for k in range(16):
                    P.mm(ps[ba][:], wav[:, k, c * 128:(c + 1) * 128], u_bf[:, k, :], k == 0, k == 15,
                         [waB, RB[k]], [psB[ba]])
                for k in range(16):
                    P.mm(ps[bb_][:], wbv[:, k, c * 128:(c + 1) * 128], u_bf[:, k, :], k == 0, k == 15,
                         [wbB, RB[k]], [psB[bb_]])
                sg, sgB = acc()
                P.act(sg[:], ps[bb_][:], AF.Sigmoid, [psB[bb_]], [sgB])
                P.tt("dve", sg[:], ps[ba][:], sg[:], ALU.mult, [psB[ba], sgB], [sgB])
                P.tt("dve", h[:, m, :], sg[:], h[:, m, :], ALU.add, [sgB, hB[m]], [hB[m]])
            WS.release(2)

    xv = xT.rearrange("(k p) t -> p k t", p=128)
    yv_ = yT.rearrange("(k p) t -> p k t", p=128)
    for t in range(ntiles):
        ts_ = slice(t * TT, (t + 1) * TT)
        cur_tile[0] = t
        for k in range(16):
            P.dma("sp", h[:, k, :], xv[:, k, ts_], xsem, [], [hB[k]])
        dump("h_in", h[:], hB)
        for l in layers:
            d = Ls[l]
            if part(l)[0]:
                emit_norm(d["nmg_t"], cB)
                if l % 2 == 0:
                    emit_mixA(l)
                else:
                    emit_mixB(l)
            if part(l)[1]:
                emit_norm(d["nfg_t"], cB)
                emit_ffn(l)
        if final:
            emit_norm(final_g, cB, to_h=True)
        for k in range(16):
            if final:
                P.dma("sp", yv_[:, k, ts_], ostage[:, k, :], osem, RB[2 * k:2 * k + 2], [])
            else:
                P.dma("sp", yv_[:, k, ts_], h[:, k, :], osem, [hB[k]], [])
    P.barrier()
    nw = P.emit()
    nc.in_names_ = in_names
    return nc, len(P.ops), nw


def _tiles_cols(w, ntile):
    K = w.shape[0] // 128
    return np.ascontiguousarray(w.reshape(K, 128, ntile, 256).transpose(2, 1, 0, 3))


def _fm(g):
    return np.ascontiguousarray(g.reshape(16, 128).T)


def _consts():
    c = np.zeros((128, 3, 128), np.float32)
    c[:, 0, :] = np.eye(128, dtype=np.float32)
    i = np.arange(128)
    c[:, 1, :] = (i[:, None] // 16 == i[None, :] // 16)
    c[:, 2, :] = (i[:, None] <= i[None, :])
    return c


def host_layer_inputs(l, inp):
    j = l // 2
    o = {}
    f32 = lambda a: np.ascontiguousarray(np.asarray(a, dtype=np.float32))
    o[f"nmg{l}"] = _fm(f32(inp["norm_mix_g"][l]))
    o[f"nfg{l}"] = _fm(f32(inp["norm_ffn_g"][l]))
    wup = f32(inp["f_w_up"][l])
    o[f"wup{l}"] = np.ascontiguousarray(
        wup.reshape(16, 128, 2, 22, 256).transpose(3, 2, 1, 0, 4)).reshape(44, 128, 16, 256)
    wdn = f32(inp["f_w_down"][l])
    o[f"wdn{l}"] = np.ascontiguousarray(wdn.reshape(44, 128, 8, 256).transpose(2, 1, 0, 3))
    cv = np.concatenate([f32(inp["f_conv_w"][l]), f32(inp["f_conv_b"][l])[None]], axis=0)
    o[f"cvw{l}"] = np.ascontiguousarray(cv.reshape(4, 88, 128).transpose(2, 1, 0))
    if l % 2 == 0:
        o[f"awin{l}"] = _tiles_cols(f32(inp["a_w_in"][j]), 16)
        o[f"awout{l}"] = _tiles_cols(f32(inp["a_w_out"][j]), 8)
        o[f"agv{l}"] = _fm(f32(inp["a_g_v"][j]))
        o[f"awsT{l}"] = np.ascontiguousarray(f32(inp["a_w_s"][j]).transpose(2, 0, 1))
        o[f"abs{l}"] = np.ascontiguousarray(np.broadcast_to(f32(inp["a_b_s"][j])[None], (128, 16, 128)))
    else:
        o[f"bwin{l}"] = _tiles_cols(f32(inp["b_w_in"][j]), 8)
        wg = f32(inp["b_w_glu"][j])
        o[f"bwglu{l}"] = np.ascontiguousarray(
            wg.reshape(16, 128, 2, 8, 256).transpose(3, 2, 1, 0, 4)).reshape(16, 128, 16, 256)
        o[f"bd{l}"] = _fm(f32(inp["b_d"][j]))
        are, aim = f32(inp["b_a_re"][j]), f32(inp["b_a_im"][j])
        ldt = f32(inp["b_log_dt"][j])
        bre, bim = f32(inp["b_b_re"][j]), f32(inp["b_b_im"][j])
        cre, cim = f32(inp["b_c_re"][j]), f32(inp["b_c_im"][j])
        ldt_gp = np.broadcast_to(ldt[:, None], (128, 64))
        a3 = np.stack([are, aim, ldt_gp], 0)
        o[f"s5a_p{l}"] = np.ascontiguousarray(a3.transpose(2, 0, 1))
        o[f"s5b_p{l}"] = np.ascontiguousarray(np.stack([bre, bim], 0).transpose(2, 0, 1, 3))
        o[f"s5c_p{l}"] = np.ascontiguousarray(np.stack([cre, cim], 0).transpose(3, 0, 1, 2))
        a_r = np.broadcast_to(a3.reshape(3, 16, 8, 1, 64), (3, 16, 8, 16, 64))
        o[f"s5a_r{l}"] = np.ascontiguousarray(a_r.transpose(2, 3, 0, 1, 4)).reshape(128, 3, 16, 64)
        b2 = np.stack([bre, bim], 0).reshape(2, 16, 4, 2, 64, 16)
        bz = np.zeros((16, 4, 2, 16, 2, 2, 64), np.float32)
        for pm in range(2):
            bz[:, :, pm, :, :, pm, :] = b2[:, :, :, pm].transpose(1, 2, 4, 0, 3)
        o[f"s5b_r{l}"] = np.ascontiguousarray(
            bz.reshape(16, 128, 2, 2, 64).transpose(1, 0, 2, 3, 4))
        a_c = a3.reshape(3, 64, 2, 64)
        o[f"s5a_c{l}"] = np.ascontiguousarray(a_c.transpose(2, 3, 0, 1)).reshape(128, 3, 64)
        c2 = np.stack([cre, cim], 0).reshape(2, 64, 2, 16, 64)
        cz = np.zeros((2, 64, 64, 2, 2, 16), np.float32)
        for pm in range(2):
            cz[pm, :, :, :, pm, :] = c2[:, :, pm].transpose(3, 1, 0, 2)
        o[f"s5c_c{l}"] = np.ascontiguousarray(cz.reshape(128, 64, 2, 32))
    return o


_CACHE = {}


def _get_prog(layers, ntok, final, parts):
    key = (tuple(layers), ntok, final, tuple(sorted(parts.items())))
    if key not in _CACHE:
        _CACHE[key] = build(list(layers), ntok, final, parts)[0]
    return _CACHE[key]


LAUNCH_GROUPS = [
    ([0, 1, 2, 3], {}),
]


def kernel(**inputs):
    x = np.asarray(inputs["x"], dtype=np.float32)
    B = x.shape[0]
    cur = [np.ascontiguousarray(x[b].T) for b in range(B)]
    cst = _consts()
    for gi, (grp, parts) in enumerate(LAUNCH_GROUPS):
        final = gi == len(LAUNCH_GROUPS) - 1
        nc = _get_prog(grp, SEQ, final, parts)
        shared = {"cst": cst}
        for l in grp:
            shared.update(host_layer_inputs(l, inputs))
        if final:
            shared["fing"] = _fm(np.asarray(inputs["final_g"], dtype=np.float32))
        in_maps = []
        shared = {k: v for k, v in shared.items() if k in nc.in_names_}
        for b in range(B):
            m = dict(shared)
            m["xT"] = cur[b]
            in_maps.append(m)
        res = run_bass_kernel_spmd(nc, in_maps, core_ids=list(range(B)))
        cur = [np.asarray(res.results[b]["yT"]) for b in range(B)]
        del shared, in_maps
    out = np.stack([c.T for c in cur], axis=0)
    return np.ascontiguousarray(out.astype(np.float32))
```

```python
import math
from contextlib import ExitStack
import numpy as np
import concourse.bass as bass
import concourse.mybir as mybir
from concourse.bass_utils import run_bass_kernel_spmd

F32 = mybir.dt.float32
BF16 = mybir.dt.bfloat16
AF = mybir.ActivationFunctionType
ALU = mybir.AluOpType

D = 2048
KC = 16
TT = 512
SEQ = 4096
FH = 5632
HC = 44
DEPTH = 4
EPS = 1e-6
NSLOT = 5
DBG = {}
SCAN_ENG2 = "pool"
ATTACH_WAITS = True
PI = math.pi


class Buf:
    __slots__ = ("name", "w", "r")

    def __init__(self, name):
        self.name = name
        self.w = None
        self.r = {}


class DSem:
    def __init__(self, nc, name):
        self.h = nc.alloc_semaphore(name)
        self.count = 0


class Op:
    __slots__ = ("eng", "fn", "edeps", "ddeps", "sig", "tok", "dsem", "idx")

    def __init__(self, eng, fn, dsem, idx):
        self.eng = eng
        self.fn = fn
        self.dsem = dsem
        self.idx = idx
        self.sig = False
        self.tok = None
        self.edeps = {}
        self.ddeps = {}


class Prog:
    ENG = ("pe", "act", "dve", "pool", "sp")

    def __init__(self, nc):
        self.nc = nc
        self.ops = []
        self.esem = {e: nc.alloc_semaphore("es_" + e) for e in self.ENG}
        self.last = {e: None for e in self.ENG}
        self.dsems = []
        self.eobj = {"pe": nc.tensor, "act": nc.scalar, "dve": nc.vector, "pool": nc.gpsimd, "sp": nc.sync}
        self._bank = 0

    def dsem(self, name):
        s = DSem(self.nc, name)
        self.dsems.append(s)
        return s

    def bank(self):
        b = self._bank
        self._bank = (b + 1) % 8
        return b

    def op(self, eng, fn, reads=(), writes=(), dsem=None, extra=()):
        o = Op(eng, fn, dsem, len(self.ops))
        isdma = dsem is not None

        def add(d, raw):
            if d.dsem is not None:
                cur = o.ddeps.get(d.dsem)
                v = d.dsem.count
                if cur is None or cur < v:
                    o.ddeps[d.dsem] = v
                return
            if d.eng == eng and not isdma:
                if eng == "pe" or not raw:
                    return
            cur = o.edeps.get(d.eng)
            if cur is None or cur.idx < d.idx:
                o.edeps[d.eng] = d

        for b in reads:
            if b.w is not None:
                add(b.w, True)
        for b in writes:
            if b.w is not None:
                add(b.w, False)
            for d in b.r.values():
                add(d, False)
        for d in extra:
            add(d, True)
        for d in o.edeps.values():
            d.sig = True
        if isdma:
            dsem.count += 16
        key = ("d", id(dsem)) if isdma else eng
        for b in writes:
            b.w = o
            b.r = {}
        for b in reads:
            if b.w is not o:
                b.r[key] = o
        self.ops.append(o)
        if not isdma and fn is not None:
            self.last[eng] = o
        return o

    def barrier(self):
        lasts = [o for o in self.last.values() if o is not None]
        for e in self.ENG:
            o = Op(e, None, None, len(self.ops))
            for d in lasts:
                if d.eng != e:
                    o.edeps[d.eng] = d
                    d.sig = True
            for s in self.dsems:
                if s.count:
                    o.ddeps[s] = s.count
            self.ops.append(o)

    def emit(self):
        seen = {e: {} for e in self.ENG}
        cnt = {e: 0 for e in self.ENG}
        nw = 0
        for o in self.ops:
            E = self.eobj[o.eng]
            sn = seen[o.eng]
            waits = []
            for d in o.edeps.values():
                sem = self.esem[d.eng]
                if sn.get(id(sem), 0) < d.tok:
                    waits.append((sem, d.tok))
                    sn[id(sem)] = d.tok
            for s, v in o.ddeps.items():
                if sn.get(id(s.h), 0) < v:
                    waits.append((s.h, v))
                    sn[id(s.h)] = v
            att = None
            if ATTACH_WAITS and o.eng != "pe" and o.fn is not None and waits and o.dsem is None:
                att = waits.pop()
            for sem, v in waits:
                E.wait_ge(sem, v)
                nw += 1
            if o.fn is None:
                continue
            ins = o.fn(E)
            if att is not None:
                ins._wait_ge(att[0], att[1])
            if o.dsem is not None:
                ins.then_inc(o.dsem.h, 16)
            elif o.sig:
                cnt[o.eng] += 1
                o.tok = cnt[o.eng]
                ins.then_inc(self.esem[o.eng], 1)
        return nw

    def mm(self, out, lhsT, rhs, start, stop, reads, writes, tp=None):
        if tp is None:
            fn = lambda E: E.matmul(out, lhsT, rhs, start=start, stop=stop)
        else:
            fn = lambda E: E.matmul(out, lhsT, rhs, start=start, stop=stop, tile_position=tp)
        return self.op("pe", fn, reads, writes)

    def act(self, out, in_, func, reads, writes, bias=None, scale=None, accum_out=None):
        kw = {}
        if bias is not None:
            kw["bias"] = bias
        if scale is not None:
            kw["scale"] = scale
        if accum_out is not None:
            kw["accum_out"] = accum_out
        return self.op("act", lambda E: E.activation(out=out, in_=in_, func=func, **kw), reads, writes)

    def tt(self, eng, out, in0, in1, op, reads, writes):
        return self.op(eng, lambda E: E.tensor_tensor(out=out, in0=in0, in1=in1, op=op), reads, writes)

    def stt(self, out, in0, scalar, in1, op0, op1, reads, writes):
        return self.op("dve", lambda E: E.scalar_tensor_tensor(out=out, in0=in0, scalar=scalar, in1=in1,
                                                               op0=op0, op1=op1), reads, writes)

    def ts(self, eng, out, in0, s1, s2, op0, op1, reads, writes):
        if s2 is None:
            fn = lambda E: E.tensor_scalar(out=out, in0=in0, scalar1=s1, scalar2=None, op0=op0)
        else:
            fn = lambda E: E.tensor_scalar(out=out, in0=in0, scalar1=s1, scalar2=s2, op0=op0, op1=op1)
        return self.op(eng, fn, reads, writes)

    def copy(self, eng, out, in_, reads, writes):
        if eng == "act":
            return self.act(out, in_, AF.Copy, reads, writes)
        return self.op(eng, lambda E: E.tensor_copy(out=out, in_=in_), reads, writes)

    def memset(self, eng, ap, val, writes):
        return self.op(eng, lambda E: E.memset(ap, val), (), writes)

    def recip(self, out, in_, reads, writes):
        return self.op("dve", lambda E: E.reciprocal(out=out, in_=in_), reads, writes)

    def dma(self, q, out, in_, dsem, reads, writes, cast=False):
        if cast:
            fn = lambda E: E.dma_start(out=out, in_=in_, max_dma_last_dim=8192)
        else:
            fn = lambda E: E.dma_start(out=out, in_=in_)
        return self.op(q, fn, reads, writes, dsem=dsem)


class Stream:
    def __init__(self, P, nc, sched):
        self.P = P
        self.sched = sched
        self.slots = [nc.alloc_sbuf_tensor(f"ws{i}", [128, 4096], BF16) for i in range(NSLOT)]
        self.bufs = [Buf(f"ws{i}") for i in range(NSLOT)]
        self.sems = [P.dsem(f"wsem{i}") for i in range(NSLOT)]
        self.nrec = 0
        self.cur = 0
        self.rel = 0

    def _rec(self, i):
        kind, src, rd = self.sched[i]
        s = i % NSLOT
        if kind == "w":
            n = src.shape[1]
            dst = self.slots[s][:].rearrange("p (k c) -> p k c", c=256)[:, 0:n, :]
            self.P.dma("pool", dst, src, self.sems[s], rd, [self.bufs[s]], cast=True)
        else:
            n = src.shape[1]
            dst = self.slots[s][:, 0:n]
            self.P.dma("sp", dst, src, self.sems[s], rd, [self.bufs[s]])

    def _fill(self):
        hi = min(self.rel + NSLOT - 1, len(self.sched) - 1)
        while self.nrec <= hi:
            self._rec(self.nrec)
            self.nrec += 1

    def get(self, kind):
        c = self.cur
        assert self.sched[c][0] == kind, (c, self.sched[c][0], kind)
        self._fill()
        assert c < self.nrec, "too many outstanding stream tiles"
        self.cur += 1
        s = c % NSLOT
        return self.slots[s], self.bufs[s]

    def release(self, n=1):
        self.rel += n
        assert self.rel <= self.cur
        self._fill()


def build(layers, ntok, final, parts=None):
    parts = parts or {}
    def part(l):
        dm, df = parts.get(l, (True, True))
        return (dm and not DBG.get("no_mix")), (df and not DBG.get("no_ffn"))
    nc = bass.Bass("TRN2", target_bir_lowering=False)
    P = Prog(nc)
    ntiles = ntok // TT

    in_names = []

    def din(name, shape, dt=F32):
        in_names.append(name)
        return nc.dram_tensor(name, list(shape), dt, kind="ExternalInput").ap()

    xT = din("xT", [D, ntok])
    yT = nc.dram_tensor("yT", [D, ntok], F32, kind="ExternalOutput").ap()
    cst = din("cst", [128, 3, 128])
    Ls = {}
    for l in layers:
        j = l // 2
        d = {}
        d["nmg"] = din(f"nmg{l}", [128, 16])
        d["nfg"] = din(f"nfg{l}", [128, 16])
        d["cvw"] = din(f"cvw{l}", [128, 88, 4])
        if part(l)[1]:
            d["wup"] = din(f"wup{l}", [44, 128, 16, 256])
            d["wdn"] = din(f"wdn{l}", [8, 128, 44, 256])
        if not part(l)[0]:
            pass
        elif l % 2 == 0:
            d["awin"] = din(f"awin{l}", [16, 128, 16, 256])
            d["awout"] = din(f"awout{l}", [8, 128, 16, 256])
            d["agv"] = din(f"agv{l}", [128, 16])
            d["awsT"] = din(f"awsT{l}", [128, 16, 128])
            d["abs"] = din(f"abs{l}", [128, 16, 128])
        else:
            d["bwin"] = din(f"bwin{l}", [8, 128, 16, 256])
            d["bwglu"] = din(f"bwglu{l}", [16, 128, 16, 256])
            d["bd"] = din(f"bd{l}", [128, 16])
            d["s5a_p"] = din(f"s5a_p{l}", [64, 3, 128])
            d["s5b_p"] = din(f"s5b_p{l}", [64, 2, 128, 16])
            d["s5c_p"] = din(f"s5c_p{l}", [64, 2, 128, 16])
            d["s5a_r"] = din(f"s5a_r{l}", [128, 3, 16, 64])
            d["s5b_r"] = din(f"s5b_r{l}", [128, 16, 2, 2, 64])
            d["s5a_c"] = din(f"s5a_c{l}", [128, 3, 64])
            d["s5c_c"] = din(f"s5c_c{l}", [128, 64, 2, 32])
            d["bs_scr"] = nc.dram_tensor(f"bs_scr{l}", [8, 128, 4096], BF16, kind="Internal").ap()
            d["kc_scr"] = nc.dram_tensor(f"kc_scr{l}", [16, 128, 3072], BF16, kind="Internal").ap()
            d["scrB"] = Buf(f"scr{l}")
        Ls[l] = d
    Alay = [l for l in layers if l % 2 == 0 and part(l)[0]]
    Blay = [l for l in layers if l % 2 == 1 and part(l)[0]]

    csem = P.dsem("csem")
    cB = Buf("consts")

    def cload(name, shape, src, dt=F32):
        t = nc.alloc_sbuf_tensor(name, list(shape), dt)
        P.dma("sp", t[:], src, csem, [], [cB])
        return t

    cst_t = cload("cst_t", [128, 3, 128], cst)
    ident = cst_t[:, 0, :]
    bdmask = cst_t[:, 1, :]
    cmask = cst_t[:, 2, :]
    ones_bf = nc.alloc_sbuf_tensor("ones_bf", [128, 128], BF16)
    P.memset("dve", ones_bf[:], 1.0, [cB])
    eps_t = nc.alloc_sbuf_tensor("eps_t", [128, 1], F32)
    P.memset("dve", eps_t[:], EPS, [cB])
    for l in layers:
        d = Ls[l]
        d["nmg_t"] = cload(f"nmg_t{l}", [128, 16], d["nmg"])
        d["nfg_t"] = cload(f"nfg_t{l}", [128, 16], d["nfg"])
        d["cvw_t"] = cload(f"cvw_t{l}", [128, 88, 4], d["cvw"])
        d["zc"] = nc.alloc_sbuf_tensor(f"zc{l}", [128, 88, 2], F32)
        d["zcB"] = Buf(f"zc{l}")
        P.memset("dve", d["zc"][:], 0.0, [d["zcB"]])
        d["E"] = nc.alloc_sbuf_tensor(f"E{l}", [128, 88, 2], F32)
        d["EB"] = Buf(f"E{l}")
        if not part(l)[0]:
            pass
        elif l % 2 == 0:
            d["agv_t"] = cload(f"agv_t{l}", [128, 16], d["agv"])
            d["wsT"] = nc.alloc_sbuf_tensor(f"wsT{l}", [128, 16, 128], BF16)
            d["wsTB"] = Buf(f"wsT{l}")
        else:
            d["scar"] = nc.alloc_sbuf_tensor(f"scar{l}", [128, 64, 2], F32)
            d["scarB"] = Buf(f"scar{l}")
            P.memset("dve", d["scar"][:], 0.0, [d["scarB"]])
            d["A2a"] = nc.alloc_sbuf_tensor(f"A2a{l}", [128, 64, 2], F32)
            d["A2b"] = nc.alloc_sbuf_tensor(f"A2b{l}", [128, 64, 2], F32)
            d["A2B"] = Buf(f"A2{l}")
    final_g = None
    if final:
        fg = din("fing", [128, 16])
        final_g = cload("fing_t", [128, 16], fg)

    scr_sem = P.dsem("scrsem")

    def red_sin(pfx, NP, shape, arg, out, pB, tmp, tmp2):
        P.copy("dve", tmp2, arg, [pB], [pB])
        for m in range(1, 6):
            P.ts("dve", tmp, arg, (2 * m - 1) * PI, -2.0 * PI, ALU.is_gt, ALU.mult, [pB], [pB])
            P.tt("dve", tmp2, tmp2, tmp, ALU.add, [pB], [pB])
        P.ts("dve", tmp2, tmp2, -PI, PI, ALU.max, ALU.min, [pB], [pB])
        P.act(out, tmp2, AF.Sin, [pB], [pB])

    def coeffs(es, pfx, NP, fshape, a_t, npow, pB):
        F = int(np.prod(fshape))
        mk = lambda n, s=None: es.enter_context(nc.sbuf_tensor(pfx + n, [NP] + list(s or [F]), F32))
        are = a_t[0:NP, 0].rearrange("p a b -> p (a b)") if len(fshape) == 2 else a_t[0:NP, 0]
        aim = a_t[0:NP, 1].rearrange("p a b -> p (a b)") if len(fshape) == 2 else a_t[0:NP, 1]
        ldt = a_t[0:NP, 2].rearrange("p a b -> p (a b)") if len(fshape) == 2 else a_t[0:NP, 2]
        dt_, mag, th, t1, t2, cs, abr, abi = [mk(n) for n in
                                             ("dt", "mag", "th", "t1", "t2", "cs", "abr", "abi")]
        sn = dt_
        Pr = mk("Pr", [npow + 1, F])
        Pi_ = mk("Pi", [npow + 1, F])
        qr, qi = mk("qr"), mk("qi")
        R = [pB]
        P.act(dt_[:], ldt, AF.Exp, [cB, pB], R)
        P.tt("dve", t1[:], dt_[:], are, ALU.mult, R, R)
        P.act(mag[:], t1[:], AF.Exp, R, R)
        P.tt("dve", th[:], dt_[:], aim, ALU.mult, R, R)
        red_sin(pfx, NP, [NP, F], th[:], sn[:], pB, t1[:], t2[:])
        P.ts("dve", th[:], th[:], PI / 2, None, ALU.add, None, R, R)
        red_sin(pfx, NP, [NP, F], th[:], cs[:], pB, t1[:], t2[:])
        P.tt("dve", abr[:], mag[:], cs[:], ALU.mult, R, R)
        P.tt("dve", abi[:], mag[:], sn[:], ALU.mult, R, R)
        P.memset("dve", Pr[:, 0, :], 1.0, R)
        P.memset("dve", Pi_[:, 0, :], 0.0, R)
        for k in range(npow):
            P.tt("dve", t1[:], Pr[:, k, :], abr[:], ALU.mult, R, R)
            P.tt("dve", t2[:], Pi_[:, k, :], abi[:], ALU.mult, R, R)
            P.tt("dve", Pr[:, k + 1, :], t1[:], t2[:], ALU.subtract, R, R)
            P.tt("dve", t1[:], Pr[:, k, :], abi[:], ALU.mult, R, R)
            P.tt("dve", t2[:], Pi_[:, k, :], abr[:], ALU.mult, R, R)
            P.tt("dve", Pi_[:, k + 1, :], t1[:], t2[:], ALU.add, R, R)
        den = th
        P.tt("dve", den[:], are, are, ALU.mult, R, R)
        P.tt("dve", t1[:], aim, aim, ALU.mult, R, R)
        P.tt("dve", den[:], den[:], t1[:], ALU.add, R, R)
        P.recip(den[:], den[:], R, R)
        am1 = mag
        P.ts("dve", am1[:], abr[:], -1.0, None, ALU.add, None, R, R)
        P.tt("dve", t1[:], am1[:], are, ALU.mult, R, R)
        P.tt("dve", t2[:], abi[:], aim, ALU.mult, R, R)
        P.tt("dve", t1[:], t1[:], t2[:], ALU.add, R, R)
        P.tt("dve", qr[:], t1[:], den[:], ALU.mult, R, R)
        P.tt("dve", t1[:], abi[:], are, ALU.mult, R, R)
        P.tt("dve", t2[:], am1[:], aim, ALU.mult, R, R)
        P.tt("dve", t1[:], t1[:], t2[:], ALU.subtract, R, R)
        P.tt("dve", qi[:], t1[:], den[:], ALU.mult, R, R)
        return Pr, Pi_, qr, qi

    def cgroup(eng, pB, t1, t2, tB, ar, ai, br, bi, o_re, o_im, neg_im, oB):
        rd = [pB, tB]
        P.tt(eng, t1, ar, br, ALU.mult, rd, [tB])
        P.tt(eng, t2, ai, bi, ALU.mult, rd, [tB])
        P.tt(eng, o_re, t1, t2, ALU.subtract, rd, [oB])
        P.tt(eng, t1, ar, bi, ALU.mult, rd, [tB])
        P.tt(eng, t2, ai, br, ALU.mult, rd, [tB])
        if not neg_im:
            P.tt(eng, o_im, t1, t2, ALU.add, rd, [oB])
        elif eng == "dve":
            P.stt(o_im, t1, -1.0, t2, ALU.mult, ALU.subtract, rd, [oB])
        else:
            P.tt(eng, t1, t1, t2, ALU.add, rd, [tB])
            P.ts(eng, o_im, t1, -1.0, None, ALU.mult, None, rd, [oB])

    def prep_A(l):
        d = Ls[l]
        pB = Buf(f"prepA{l}")
        P.barrier()
        with nc.sbuf_tensor(f"wsraw{l}", [128, 16, 128], F32) as raw:
            P.dma("sp", raw[:], d["awsT"], scr_sem, [], [pB])
            P.tt("dve", d["wsT"][:], raw[:], cmask.unsqueeze(1).to_broadcast([128, 16, 128]), ALU.mult,
                 [pB, cB], [d["wsTB"]])

    def prep_B(l):
        d = Ls[l]
        pB = Buf(f"prepB{l}")
        R = [pB]
        bs_scr, kc_scr, scrB = d["bs_scr"], d["kc_scr"], d["scrB"]
        P.barrier()
        with ExitStack() as es:
            sa = lambda n, sh, dt=F32: es.enter_context(nc.sbuf_tensor(n + str(l), sh, dt))
            a_t = sa("l3a", [128, 3, 64])
            c_t = sa("l3c", [128, 64, 2, 32])
            o0 = sa("l3o0", [128, 16, 8, 2, 32], BF16)
            o1 = sa("l3o1", [128, 16, 8, 2, 32], BF16)
            t1 = sa("l3t1", [128, 16, 32])
            t2 = sa("l3t2", [128, 16, 32])
            P.dma("sp", a_t[:], d["s5a_c"], scr_sem, [], R)
            P.dma("sp", c_t[:], d["s5c_c"], scr_sem, [], R)
            Pr, Pi_, qr, qi = coeffs(es, f"l3_{l}", 128, [64], a_t, 8, pB)
            for c in range(2):
                P.copy("dve", d["A2a"][:, :, c], Pr[:, 8, :], R, [d["A2B"]])
            P.ts("dve", d["A2b"][:, :, 0], Pi_[:, 8, :], -1.0, None, ALU.mult, None, R, [d["A2B"]])
            P.copy("dve", d["A2b"][:, :, 1], Pi_[:, 8, :], R, [d["A2B"]])
            t3 = sa("l3t3", [128, 16, 32])
            t4 = sa("l3t4", [128, 16, 32])
            tmps = {"dve": (t1, t2, Buf("l3tA")), "pool": (t3, t4, Buf("l3tB"))}
            obufs = [{"dve": Buf("l3o0d"), "pool": Buf("l3o0p")}, {"dve": Buf("l3o1d"), "pool": Buf("l3o1p")}]
            for pg in range(4):
                o = (o0, o1)[pg % 2]
                oB = obufs[pg % 2]
                cr = c_t[:, pg * 16:(pg + 1) * 16, 0, :]
                ci = c_t[:, pg * 16:(pg + 1) * 16, 1, :]
                for r in range(8):
                    eng = "pool" if r % 4 == 3 else "dve"
                    ta, tb_, tB = tmps[eng]
                    prb = Pr[:, r + 1, pg * 16:(pg + 1) * 16].unsqueeze(2).to_broadcast([128, 16, 32])
                    pib = Pi_[:, r + 1, pg * 16:(pg + 1) * 16].unsqueeze(2).to_broadcast([128, 16, 32])
                    cgroup(eng, pB, ta[:], tb_[:], tB, cr, ci, prb, pib, o[:, :, r, 0, :], o[:, :, r, 1, :],
                           True, oB[eng])
                dst = kc_scr[4 * pg:4 * pg + 4, :, 1024:3072].rearrange("f p (k x) -> p f k x", k=4)
                src = o[:].rearrange("p (f k) r e c -> p f k (r e c)", k=4)
                P.dma("sp", dst, src, scr_sem, [oB["dve"], oB["pool"]], [scrB])
        P.barrier()
        with ExitStack() as es:
            sa = lambda n, sh, dt=F32: es.enter_context(nc.sbuf_tensor(n + str(l), sh, dt))
            a_t = sa("l2a", [128, 3, 16, 64])
            b_t = sa("l2b", [128, 16, 2, 2, 64])
            bb = sa("l2bb", [128, 2, 16, 2, 64])
            o0 = sa("l2o0", [128, 2, 8, 2, 128], BF16)
            o1 = o0
            t1 = sa("l2t1", [128, 16, 2, 64])
            t2 = sa("l2t2", [128, 16, 2, 64])
            P.dma("sp", a_t[:], d["s5a_r"], scr_sem, [], R)
            P.dma("sp", b_t[:], d["s5b_r"], scr_sem, [], R)
            Pr, Pi_, qr, qi = coeffs(es, f"l2_{l}", 128, [16, 64], a_t, 7, pB)
            bc = lambda t: t.rearrange("p (f x) -> p f x", x=64).unsqueeze(2).to_broadcast([128, 16, 2, 64])
            br, bi = b_t[:, :, 0], b_t[:, :, 1]
            P.tt("dve", t1[:], br, bc(qr[:]), ALU.mult, R, R)
            P.tt("dve", t2[:], bi, bc(qi[:]), ALU.mult, R, R)
            P.tt("dve", bb[:, 0], t1[:], t2[:], ALU.subtract, R, R)
            P.tt("dve", t1[:], bi, bc(qr[:]), ALU.mult, R, R)
            P.tt("dve", t2[:], br, bc(qi[:]), ALU.mult, R, R)
            P.tt("dve", bb[:, 1], t1[:], t2[:], ALU.add, R, R)
            t3 = sa("l2t3", [128, 2, 2, 64])
            t4 = sa("l2t4", [128, 2, 2, 64])
            tmps = {"dve": (t1[:, 0:2], t2[:, 0:2], Buf("l2tA")), "pool": (t3[:], t4[:], Buf("l2tB"))}
            oB = {"dve": Buf("l2od"), "pool": Buf("l2op")}
            for f2 in range(8):
                o = o0
                fs = slice(2 * f2, 2 * f2 + 2)
                ov = o[:].rearrange("p f q e (m x) -> p f q e m x", m=2)
                for q in range(8):
                    eng = "pool" if q % 4 == 3 else "dve"
                    ta, tb_, tB = tmps[eng]
                    k = 7 - q
                    pr = bc(Pr[:, k, :])[:, fs]
                    pi = bc(Pi_[:, k, :])[:, fs]
                    cgroup(eng, pB, ta, tb_, tB, bb[:, 0, fs], bb[:, 1, fs], pr, pi, ov[:, :, q, 0], ov[:, :, q, 1],
                           False, oB[eng])
                P.dma("sp", bs_scr[f2], o[:].rearrange("p f q e x -> p (f q e x)"), scr_sem,
                      [oB["dve"], oB["pool"]], [scrB])
        P.barrier()
        with ExitStack() as es:
            sa = lambda n, sh, dt=F32: es.enter_context(nc.sbuf_tensor(n + str(l), sh, dt))
            a_t = sa("l1a", [64, 3, 128])
            b_t = sa("l1b", [64, 2, 128, 16])
            c_t = sa("l1c", [64, 2, 128, 16])
            bb = sa("l1bb", [64, 2, 128, 16])
            cp = sa("l1cp", [64, 2, 8, 4, 128])
            t1 = sa("l1t1", [64, 128, 16])
            t2 = sa("l1t2", [64, 128, 16])
            kt = sa("l1k", [128, 8, 128])
            o0 = sa("l1o0", [128, 8, 128], BF16)
            o1 = sa("l1o1", [128, 8, 128], BF16)
            P.dma("sp", a_t[:], d["s5a_p"], scr_sem, [], R)
            P.dma("sp", b_t[:], d["s5b_p"], scr_sem, [], R)
            P.dma("sp", c_t[:], d["s5c_p"], scr_sem, [], R)
            Pr, Pi_, qr, qi = coeffs(es, f"l1_{l}", 64, [128], a_t, 7, pB)
            bc = lambda t, n=128: t.unsqueeze(2).to_broadcast([64, n, 16])
            br, bi = b_t[:, 0], b_t[:, 1]
            P.tt("dve", t1[:], br, bc(qr[:]), ALU.mult, R, R)
            P.tt("dve", t2[:], bi, bc(qi[:]), ALU.mult, R, R)
            P.tt("dve", bb[:, 0], t1[:], t2[:], ALU.subtract, R, R)
            P.tt("dve", t1[:], bi, bc(qr[:]), ALU.mult, R, R)
            P.tt("dve", t2[:], br, bc(qi[:]), ALU.mult, R, R)
            P.tt("dve", bb[:, 1], t1[:], t2[:], ALU.add, R, R)
            obufs = [Buf("l1o0"), Buf("l1o1")]
            ktB = Buf("l1kt")
            t3 = sa("l1t3", [64, 32, 16])
            t4 = sa("l1t4", [64, 32, 16])
            tmps = {"dve": (t1[:, 0:32], t2[:, 0:32], Buf("l1tA")), "pool": (t3[:], t4[:], Buf("l1tB"))}
            cpB = {"dve": Buf("l1cpd"), "pool": Buf("l1cpp")}
            CPR = [pB, cpB["dve"], cpB["pool"]]
            for fg in range(4):
                gs = slice(32 * fg, 32 * fg + 32)
                cr, ci = c_t[:, 0, gs], c_t[:, 1, gs]
                for tau in range(8):
                    eng = "pool" if tau % 4 == 3 else "dve"
                    ta, tb_, tB = tmps[eng]
                    prb = bc(Pr[:, tau, gs], 32)
                    pib = bc(Pi_[:, tau, gs], 32)
                    cpr = cp[:, 0, tau].rearrange("p f (g c) -> p (f g) c", c=16)
                    cpi = cp[:, 1, tau].rearrange("p f (g c) -> p (f g) c", c=16)
                    cgroup(eng, pB, ta, tb_, tB, cr, ci, prb, pib, cpr, cpi, True, cpB[eng])
                for fl in range(4):
                    fc = 4 * fg + fl
                    o = (o0, o1)[fc % 2]
                    oB = obufs[fc % 2]
                    lre = bb[:, 0, 8 * fc:8 * fc + 8, :].rearrange("p g c -> p (g c)")
                    lim = bb[:, 1, 8 * fc:8 * fc + 8, :].rearrange("p g c -> p (g c)")
                    for hf in range(2):
                        bk = P.bank()
                        rre = cp[:, 0, 4 * hf:4 * hf + 4, fl, :]
                        rim = cp[:, 1, 4 * hf:4 * hf + 4, fl, :]
                        pso = ps[bk][:].rearrange("p (t x) -> p t x", x=128)
                        P.mm(pso, lre, rre, True, False, CPR, [psB[bk]])
                        P.mm(pso, lim, rim, False, True, CPR, [psB[bk]])
                        P.tt("dve", kt[:, 4 * hf:4 * hf + 4, :], ps[bk][:].rearrange("p (t x) -> p t x", x=128),
                             bdmask.unsqueeze(1).to_broadcast([128, 4, 128]), ALU.mult, [psB[bk], cB], [ktB])
                    P.stt(kt[:, 0, :], ident, d["bd_t"][:, fc:fc + 1], kt[:, 0, :], ALU.mult, ALU.add,
                          [ktB, cB], [ktB])
                    P.copy("dve", o[:], kt[:], [ktB], [oB])
                    P.dma("sp", kc_scr[fc, :, 0:1024], o[:].rearrange("p t x -> p (t x)"), scr_sem, [oB], [scrB])

    ps = [nc.alloc_psum_tensor(f"ps{i}", [128, 512], F32) for i in range(8)]
    psB = [Buf(f"ps{i}") for i in range(8)]
    for l in Blay:
        Ls[l]["bd_t"] = cload(f"bd_t{l}", [128, 16], Ls[l]["bd"])
    for l in Alay:
        if part(l)[0]:
            prep_A(l)
    for l in Blay:
        if part(l)[0]:
            prep_B(l)
    P.barrier()

    h = nc.alloc_sbuf_tensor("h", [128, 16, TT], F32)
    hB = [Buf(f"h{k}") for k in range(16)]
    hn = nc.alloc_sbuf_tensor("hn", [128, 16, TT], BF16)
    hnB = [Buf(f"hn{k}") for k in range(16)]
    U = nc.alloc_sbuf_tensor("U", [128, 16384], F32)
    RB = [Buf(f"U{i}") for i in range(64)]
    a_bf = U[:, 0:11264].bitcast(BF16).rearrange("p (c t) -> p c t", t=TT)
    u_bf = U[:, 0:4096].bitcast(BF16).rearrange("p (c t) -> p c t", t=TT)
    v_bf = U[:, 4096:8192].bitcast(BF16).rearrange("p (b f) -> p b f", f=2048)
    vn_bf = U[:, 8192:12288].bitcast(BF16).rearrange("p (b f) -> p b f", f=2048)
    XS = U[:, 4096:12288].rearrange("p (a r j) -> p a r j", r=2, j=64)
    Sb = U[:, 12288:16384].bitcast(BF16).rearrange("p (a r j) -> p a r j", r=2, j=64)
    XSB = RB[16:48]
    SbB = RB[48:64]
    ostage = U[:, 0:8192].rearrange("p (c t) -> p c t", t=TT)
    NACC = 4
    accs = [nc.alloc_sbuf_tensor(f"acc{i}", [128, TT], F32) for i in range(NACC)]
    accB = [Buf(f"acc{i}") for i in range(NACC)]
    acc_i = [0]

    def acc():
        i = acc_i[0]
        acc_i[0] = (i + 1) % NACC
        return accs[i], accB[i]

    small = nc.alloc_sbuf_tensor("small", [128, 16], F32)
    smallB = Buf("small")
    stmp = nc.alloc_sbuf_tensor("stmp", [128, 4, 64, 2], F32)
    stB = [Buf("stA0"), Buf("stB0"), Buf("stA1"), Buf("stB1")]
    abb = nc.alloc_sbuf_tensor("abb", [128, 16, 128], F32)
    abbB = Buf("abb")
    abb_sem = P.dsem("abbsem")
    xsem = P.dsem("xsem")
    osem = P.dsem("osem")

    sched = []
    for t in range(ntiles):
        for l in layers:
            d = Ls[l]
            if not part(l)[0]:
                pass
            elif l % 2 == 0:
                sched += [("w", d["awin"][i], []) for i in range(16)]
                sched += [("w", d["awout"][i], []) for i in range(8)]
            else:
                sched += [("w", d["bwin"][i], []) for i in range(8)]
                sched += [("t", d["bs_scr"][i], [d["scrB"]]) for i in range(8)]
                sched += [("t", d["kc_scr"][i], [d["scrB"]]) for i in range(16)]
                sched += [("w", d["bwglu"][i], []) for i in range(16)]
            if not part(l)[1]:
                continue
            sched += [("w", d["wup"][i], []) for i in range(44)]
            for mg in range(8):
                for (h0, n) in ((0, 16), (16, 16), (32, 12)):
                    sched.append(("w", d["wdn"][mg, :, h0:h0 + n, :], []))
    WS = Stream(P, nc, sched)

    def wview(slot):
        return slot[:].rearrange("p (k c) -> p k c", c=256)

    dsem_dbg = P.dsem("dbgsem")
    cur_tile = [0]

    def dump(name, ap, bufs):
        if DBG.get("dump_tile") != cur_tile[0] or name not in DBG.get("dump", ()):
            return
        t_ = nc.dram_tensor("dbg_" + name, list(ap.shape), ap.dtype, kind="ExternalOutput").ap()
        P.dma("sp", t_, ap, dsem_dbg, list(bufs), [])

    def emit_norm(g_t, gB, to_h=False):
        bk = P.bank()
        for k in range(16):
            if k % 2 == 0:
                P.act(hn[:, k, :], h[:, k, :], AF.Square, [hB[k]], [hnB[k]])
            else:
                P.tt("dve", hn[:, k, :], h[:, k, :], h[:, k, :], ALU.mult, [hB[k]], [hnB[k]])
            P.mm(ps[bk][:], ones_bf[:], hn[:, k, :], k == 0, k == 15, [hnB[k], cB], [psB[bk]])
        rs, rsB = acc()
        P.act(rs[:], ps[bk][:], AF.Sqrt, [psB[bk], cB], [rsB], bias=eps_t[:], scale=1.0 / D)
        P.recip(rs[:], rs[:], [rsB], [rsB])
        dump("rs", rs[:], [rsB])
        for k in range(16):
            if to_h:
                P.stt(ostage[:, k, :], h[:, k, :], g_t[:, k:k + 1], rs[:], ALU.mult, ALU.mult, [hB[k], rsB, gB],
                      RB[2 * k:2 * k + 2])
            else:
                P.stt(hn[:, k, :], h[:, k, :], g_t[:, k:k + 1], rs[:], ALU.mult, ALU.mult, [hB[k], rsB, gB],
                      [hnB[k]])

    def proj_fm(nchunks_pairs, rd_act, rdB, evac):
        for t in range(nchunks_pairs):
            w, wB = WS.get("w")
            wv = wview(w)
            for c in range(2):
                m = 2 * t + c
                bk = P.bank()
                for k in range(16):
                    P.mm(ps[bk][:], wv[:, k, c * 128:(c + 1) * 128], rd_act[:, k, :], k == 0, k == 15,
                         [wB, rdB(k)], [psB[bk]])
                evac(m, bk)
            WS.release()

    def resid_add(m, bk):
        P.tt("dve", h[:, m, :], ps[bk][:], h[:, m, :], ALU.add, [psB[bk], hB[m]], [hB[m]])

    def emit_ffn(l):
        d = Ls[l]
        cv, zc, zcB, E, EB = d["cvw_t"], d["zc"], d["zcB"], d["E"], d["EB"]
        P.tt("dve", E[:, :, 0], zc[:, :, 1], cv[:, :, 1], ALU.mult, [zcB, cB], [EB])
        P.tt("dve", E[:, :, 1], zc[:, :, 0], cv[:, :, 0], ALU.mult, [zcB, cB], [EB])
        P.tt("dve", E[:, :, 0], E[:, :, 0], E[:, :, 1], ALU.add, [EB], [EB])
        P.tt("dve", E[:, :, 1], zc[:, :, 1], cv[:, :, 0], ALU.mult, [zcB, cB, EB], [EB])

        def conv(bk, ch, a_t, aB):
            p = ps[bk]
            P.act(a_t[:], p[:], AF.Identity, [psB[bk], cB], [aB], bias=cv[:, ch, 3:4], scale=cv[:, ch, 2:3])
            P.act(zc[:, ch, :], p[:, 510:512], AF.Copy, [psB[bk]], [zcB])
            P.stt(a_t[:, 1:512], p[:, 0:511], cv[:, ch, 1:2], a_t[:, 1:512], ALU.mult, ALU.add,
                  [psB[bk], aB, cB], [aB])
            P.stt(a_t[:, 2:512], p[:, 0:510], cv[:, ch, 0:1], a_t[:, 2:512], ALU.mult, ALU.add,
                  [psB[bk], aB, cB], [aB])
            P.tt("dve", a_t[:, 0:2], a_t[:, 0:2], E[:, ch, :], ALU.add, [aB, EB], [aB])

        for pg in range(22):
            wg, wgB = WS.get("w")
            wv_, wvB = WS.get("w")
            wgv, wvv = wview(wg), wview(wv_)
            for c in range(2):
                hc = 2 * pg + c
                bg = P.bank()
                bv = P.bank()
                for k in range(16):
                    P.mm(ps[bg][:], wgv[:, k, c * 128:(c + 1) * 128], hn[:, k, :], k == 0, k == 15,
                         [wgB, hnB[k]], [psB[bg]])
                for k in range(16):
                    P.mm(ps[bv][:], wvv[:, k, c * 128:(c + 1) * 128], hn[:, k, :], k == 0, k == 15,
                         [wvB, hnB[k]], [psB[bv]])
                ag, agB = acc()
                av, avB = acc()
                conv(bg, hc, ag, agB)
                conv(bv, 44 + hc, av, avB)
                P.act(ag[:], ag[:], AF.Silu, [agB], [agB])
                P.tt("dve", a_bf[:, hc, :], ag[:], av[:], ALU.mult, [agB, avB], [RB[hc]])
            WS.release(2)
        for mg in range(8):
            b0 = P.bank()
            b1 = P.bank()
            bks = (b0, b1)
            for (h0, n) in ((0, 16), (16, 16), (32, 12)):
                w, wB = WS.get("w")
                wv = wview(w)
                for i in range(n):
                    hc = h0 + i
                    for c in range(2):
                        P.mm(ps[bks[c]][:], wv[:, i, c * 128:(c + 1) * 128], a_bf[:, hc, :], hc == 0, hc == 43,
                             [wB, RB[hc]], [psB[bks[c]]])
                WS.release()
            for c in range(2):
                resid_add(2 * mg + c, bks[c])

    def emit_mixA(l):
        d = Ls[l]
        gv, wsT, wsTB = d["agv_t"], d["wsT"], d["wsTB"]
        P.dma("sp", abb[:], d["abs"], abb_sem, [], [abbB])

        def ev_u(m, bk):
            P.act(u_bf[:, m, :], ps[bk][:], AF.Gelu_apprx_tanh, [psB[bk]], [RB[m]])

        dump("hn", hn[:], hnB)
        proj_fm(8, hn, lambda k: hnB[k], ev_u)
        dump("u", u_bf, RB[0:16])
        for t in range(8):
            w, wB = WS.get("w")
            wv = wview(w)
            bks = (P.bank(), P.bank())
            for tb in range(4):
                bk = bks[tb // 2]
                o = ps[bk][:, (tb % 2) * 256:(tb % 2 + 1) * 256]
                for k in range(16):
                    P.mm(o, hn[:, k, tb * 128:(tb + 1) * 128], wv[:, k, :], k == 0, k == 15,
                         [wB, hnB[k]], [psB[bk]])
            for hf in range(2):
                bk = bks[hf]
                P.act(v_bf[:, 2 * hf:2 * hf + 2, t * 256:(t + 1) * 256],
                      ps[bk][:].rearrange("p (a c) -> p a c", a=2), AF.Gelu_apprx_tanh,
                      [psB[bk]], RB[16 + 8 * hf:16 + 8 * hf + 8])
            WS.release()
        dump("v", v_bf, RB[16:32])
        ssq = small[:, 0:4]
        rsv = small[:, 4:8]
        for tb in range(4):
            P.act(vn_bf[:, tb, :], v_bf[:, tb, :], AF.Square, RB[16 + 4 * tb:20 + 4 * tb],
                  RB[32 + 4 * tb:36 + 4 * tb] + [smallB], accum_out=small[:, tb:tb + 1])
        P.act(rsv, ssq, AF.Sqrt, [smallB, cB], [smallB], bias=eps_t[:], scale=1.0 / D)
        P.recip(rsv, rsv, [smallB], [smallB])
        for tb in range(4):
            P.ts("dve", vn_bf[:, tb, :], v_bf[:, tb, :], small[:, 4 + tb:5 + tb], None, ALU.mult, None,
                 RB[16 + 4 * tb:20 + 4 * tb] + [smallB], RB[32 + 4 * tb:36 + 4 * tb])
        dump("vn", vn_bf, RB[32:48])
        dump("small", small[:], [smallB])
        for hd in range(16):
            bk = P.bank()
            for tb in range(4):
                P.mm(ps[bk][:, tb * 128:(tb + 1) * 128], vn_bf[:, tb, hd * 128:(hd + 1) * 128], wsT[:, hd, :],
                     True, True, RB[32 + 4 * tb:36 + 4 * tb] + [wsTB], [psB[bk]])
            tmp, tmpB = acc()
            P.stt(tmp[:].rearrange("p (a t) -> p a t", a=4), ps[bk][:].rearrange("p (a t) -> p a t", a=4),
                  gv[:, hd:hd + 1], abb[:, hd, :].unsqueeze(1).to_broadcast([128, 4, 128]), ALU.mult, ALU.add,
                  [psB[bk], cB, abbB], [tmpB])
            P.tt("dve", u_bf[:, hd, :], tmp[:], u_bf[:, hd, :], ALU.mult, [tmpB, RB[hd]], [RB[hd]])
        dump("us", u_bf, RB[0:16])
        proj_fm(8, u_bf, lambda k: RB[k], resid_add)

    def emit_mixB(l):
        d = Ls[l]
        scar, scarB, A2a, A2b, A2B = d["scar"], d["scarB"], d["A2a"], d["A2b"], d["A2B"]

        def ev_u(m, bk):
            P.act(u_bf[:, m, :], ps[bk][:], AF.Copy, [psB[bk]], [RB[m]])

        proj_fm(8, hn, lambda k: hnB[k], ev_u)
        XSv = U[:, 4096:12288].rearrange("p (a k r j) -> p k a (r j)", k=4, r=2, j=64)
        bsv = None
        for fg in range(4):
            bx = [P.bank() for _ in range(4)]
            for fl in range(4):
                fc = 4 * fg + fl
                if fc % 2 == 0:
                    bs, bsB = WS.get("t")
                    bsv = bs[:].rearrange("p (f q e x) -> p f q e x", f=2, q=8, e=2)
                uq = u_bf[:, fc, :].rearrange("p (j q) -> p q j", q=8)
                for e in range(2):
                    for q in range(8):
                        for k in range(4):
                            P.mm(ps[bx[k]][:, fl * 128 + e * 64:fl * 128 + e * 64 + 64],
                                 bsv[32 * k:32 * k + 32, fc % 2, q, e, :], uq[32 * k:32 * k + 32, q, :],
                                 q == 0, q == 7, [bsB, RB[fc]], [psB[bx[k]]], tp=(32 * k, 0))
                if fc % 2 == 1:
                    WS.release()
            for k in range(4):
                P.copy("act", XSv[:, k, 4 * fg:4 * fg + 4, :],
                       ps[bx[k]][:].rearrange("p (a x) -> p a x", a=4), [psB[bx[k]]], XSB[8 * fg:8 * fg + 8])
        tA = stmp[:, 0, :, :]
        tB = stmp[:, 1, :, :]
        tAB, tBB = stB[0], stB[1]
        for j in range(64):
            if j == 0:
                sp_, spr, spB = scar[:], scar[:, :, ::-1], [scarB]
            else:
                sp_, spr, spB = XS[:, :, :, j - 1], XS[:, :, ::-1, j - 1], XSB
            cur = XS[:, :, :, j]
            P.tt("dve", tA, sp_, A2a[:], ALU.mult, spB + [A2B], [tAB])
            P.tt("dve", tB, spr, A2b[:], ALU.mult, spB + [A2B], [tBB])
            P.tt("dve", cur, cur, tA, ALU.add, XSB + [tAB], XSB)
            P.tt("dve", cur, cur, tB, ALU.add, XSB + [tBB], XSB)
        P.copy("act", Sb[:, :, :, 1:64], XS[:, :, :, 0:63], XSB, SbB)
        P.copy("act", Sb[:, :, :, 0], scar[:], [scarB], SbB)
        P.copy("act", scar[:], XS[:, :, :, 63], XSB, [scarB])
        for fc in range(16):
            kc, kcB = WS.get("t")
            kd = kc[:, 0:1024].rearrange("p (t x) -> p t x", x=128)
            cs = kc[:, 1024:3072].rearrange("p (k r e c) -> p k r e c", k=4, r=8, e=2)
            bk = P.bank()
            yv = ps[bk][:].rearrange("p (j r) -> p j r", r=8)
            yr = ps[bk][:].rearrange("p (j r) -> p r j", r=8)
            uv = u_bf[:, fc, :].rearrange("p (j r) -> p j r", r=8)
            for tau in range(8):
                P.mm(yv[:, :, tau:8], kd[:, tau, :], uv[:, :, 0:8 - tau], tau == 0, False,
                     [kcB, RB[fc]], [psB[bk]])
            for r in range(8):
                for e in range(2):
                    for k in range(4):
                        pair = 4 * fc + k
                        P.mm(yr[32 * k:32 * k + 32, r, :], cs[:, k, r, e, :], Sb[:, pair, e, :], False,
                             (r == 7 and e == 1 and k == 3), [kcB] + SbB, [psB[bk]], tp=(0, 32 * k))
            WS.release()
            P.act(u_bf[:, fc, :], ps[bk][:], AF.Gelu_apprx_tanh, [psB[bk]], [RB[fc]])
        for mg in range(8):
            wa, waB = WS.get("w")
            wb, wbB = WS.get("w")
            wav, wbv = wview(wa), wview(wb)
            for c in range(2):
                m = 2 * mg + c
                ba = P.bank()
                bb_ = P.bank()
                for k in range(16):
                    P.mm(ps[ba][:], wav[:, k, c * 128:(c + 1) * 128], u_bf[:, k, :], k == 0, k == 15,
                         [waB, RB[k]], [psB[ba]])
                for k in range(16):
                    P.mm(ps[bb_][:], wbv[:, k, c * 128:(c + 1) * 128], u_bf[:, k, :], k == 0, k == 15,
                         [wbB, RB[k]], [psB[bb_]])
                sg, sgB = acc()
                P.act(sg[:], ps[bb_][:], AF.Sigmoid, [psB[bb_]], [sgB])
                P.tt("dve", sg[:], ps[ba][:], sg[:], ALU.mult, [psB[ba], sgB], [sgB])
                P.tt("dve", h[:, m, :], sg[:], h[:, m, :], ALU.add, [sgB, hB[m]], [hB[m]])
            WS.release(2)

    xv = xT.rearrange("(k p) t -> p k t", p=128)
    yv_ = yT.rearrange("(k p) t -> p k t", p=128)
    for t in range(ntiles):
        ts_ = slice(t * TT, (t + 1) * TT)
        cur_tile[0] = t
        for k in range(16):
            P.dma("sp", h[:, k, :], xv[:, k, ts_], xsem, [], [hB[k]])
        dump("h_in", h[:], hB)
        for l in layers:
            d = Ls[l]
            if part(l)[0]:
                emit_norm(d["nmg_t"], cB)
                if l % 2 == 0:
                    emit_mixA(l)
                else:
                    emit_mixB(l)
            if part(l)[1]:
                emit_norm(d["nfg_t"], cB)
                emit_ffn(l)
        if final:
            emit_norm(final_g, cB, to_h=True)
        for k in range(16):
            if final:
                P.dma("sp", yv_[:, k, ts_], ostage[:, k, :], osem, RB[2 * k:2 * k + 2], [])
            else:
                P.dma("sp", yv_[:, k, ts_], h[:, k, :], osem, [hB[k]], [])
    P.barrier()
    nw = P.emit()
    nc.in_names_ = in_names
    return nc, len(P.ops), nw


def _tiles_cols(w, ntile):
    K = w.shape[0] // 128
    return np.ascontiguousarray(w.reshape(K, 128, ntile, 256).transpose(2, 1, 0, 3))


def _fm(g):
    return np.ascontiguousarray(g.reshape(16, 128).T)


def _consts():
    c = np.zeros((128, 3, 128), np.float32)
    c[:, 0, :] = np.eye(128, dtype=np.float32)
    i = np.arange(128)
    c[:, 1, :] = (i[:, None] // 16 == i[None, :] // 16)
    c[:, 2, :] = (i[:, None] <= i[None, :])
    return c


def host_layer_inputs(l, inp):
    j = l // 2
    o = {}
    f32 = lambda a: np.ascontiguousarray(np.asarray(a, dtype=np.float32))
    o[f"nmg{l}"] = _fm(f32(inp["norm_mix_g"][l]))
    o[f"nfg{l}"] = _fm(f32(inp["norm_ffn_g"][l]))
    wup = f32(inp["f_w_up"][l])
    o[f"wup{l}"] = np.ascontiguousarray(
        wup.reshape(16, 128, 2, 22, 256).transpose(3, 2, 1, 0, 4)).reshape(44, 128, 16, 256)
    wdn = f32(inp["f_w_down"][l])
    o[f"wdn{l}"] = np.ascontiguousarray(wdn.reshape(44, 128, 8, 256).transpose(2, 1, 0, 3))
    cv = np.concatenate([f32(inp["f_conv_w"][l]), f32(inp["f_conv_b"][l])[None]], axis=0)
    o[f"cvw{l}"] = np.ascontiguousarray(cv.reshape(4, 88, 128).transpose(2, 1, 0))
    if l % 2 == 0:
        o[f"awin{l}"] = _tiles_cols(f32(inp["a_w_in"][j]), 16)
        o[f"awout{l}"] = _tiles_cols(f32(inp["a_w_out"][j]), 8)
        o[f"agv{l}"] = _fm(f32(inp["a_g_v"][j]))
        o[f"awsT{l}"] = np.ascontiguousarray(f32(inp["a_w_s"][j]).transpose(2, 0, 1))
        o[f"abs{l}"] = np.ascontiguousarray(np.broadcast_to(f32(inp["a_b_s"][j])[None], (128, 16, 128)))
    else:
        o[f"bwin{l}"] = _tiles_cols(f32(inp["b_w_in"][j]), 8)
        wg = f32(inp["b_w_glu"][j])
        o[f"bwglu{l}"] = np.ascontiguousarray(
            wg.reshape(16, 128, 2, 8, 256).transpose(3, 2, 1, 0, 4)).reshape(16, 128, 16, 256)
        o[f"bd{l}"] = _fm(f32(inp["b_d"][j]))
        are, aim = f32(inp["b_a_re"][j]), f32(inp["b_a_im"][j])
        ldt = f32(inp["b_log_dt"][j])
        bre, bim = f32(inp["b_b_re"][j]), f32(inp["b_b_im"][j])
        cre, cim = f32(inp["b_c_re"][j]), f32(inp["b_c_im"][j])
        ldt_gp = np.broadcast_to(ldt[:, None], (128, 64))
        a3 = np.stack([are, aim, ldt_gp], 0)
        o[f"s5a_p{l}"] = np.ascontiguousarray(a3.transpose(2, 0, 1))
        o[f"s5b_p{l}"] = np.ascontiguousarray(np.stack([bre, bim], 0).transpose(2, 0, 1, 3))
        o[f"s5c_p{l}"] = np.ascontiguousarray(np.stack([cre, cim], 0).transpose(3, 0, 1, 2))
        a_r = np.broadcast_to(a3.reshape(3, 16, 8, 1, 64), (3, 16, 8, 16, 64))
        o[f"s5a_r{l}"] = np.ascontiguousarray(a_r.transpose(2, 3, 0, 1, 4)).reshape(128, 3, 16, 64)
        b2 = np.stack([bre, bim], 0).reshape(2, 16, 4, 2, 64, 16)
        bz = np.zeros((16, 4, 2, 16, 2, 2, 64), np.float32)
        for pm in range(2):
            bz[:, :, pm, :, :, pm, :] = b2[:, :, :, pm].transpose(1, 2, 4, 0, 3)
        o[f"s5b_r{l}"] = np.ascontiguousarray(
            bz.reshape(16, 128, 2, 2, 64).transpose(1, 0, 2, 3, 4))
        a_c = a3.reshape(3, 64, 2, 64)
        o[f"s5a_c{l}"] = np.ascontiguousarray(a_c.transpose(2, 3, 0, 1)).reshape(128, 3, 64)
        c2 = np.stack([cre, cim], 0).reshape(2, 64, 2, 16, 64)
        cz = np.zeros((2, 64, 64, 2, 2, 16), np.float32)
        for pm in range(2):
            cz[pm, :, :, :, pm, :] = c2[:, :, pm].transpose(3, 1, 0, 2)
        o[f"s5c_c{l}"] = np.ascontiguousarray(cz.reshape(128, 64, 2, 32))
    return o


_CACHE = {}


def _get_prog(layers, ntok, final, parts):
    key = (tuple(layers), ntok, final, tuple(sorted(parts.items())))
    if key not in _CACHE:
        _CACHE[key] = build(list(layers), ntok, final, parts)[0]
    return _CACHE[key]


LAUNCH_GROUPS = [
    ([0, 1, 2, 3], {}),
]


def kernel(**inputs):
    x = np.asarray(inputs["x"], dtype=np.float32)
    B = x.shape[0]
    cur = [np.ascontiguousarray(x[b].T) for b in range(B)]
    cst = _consts()
    for gi, (grp, parts) in enumerate(LAUNCH_GROUPS):
        final = gi == len(LAUNCH_GROUPS) - 1
        nc = _get_prog(grp, SEQ, final, parts)
        shared = {"cst": cst}
        for l in grp:
            shared.update(host_layer_inputs(l, inputs))
        if final:
            shared["fing"] = _fm(np.asarray(inputs["final_g"], dtype=np.float32))
        in_maps = []
        shared = {k: v for k, v in shared.items() if k in nc.in_names_}
        for b in range(B):
            m = dict(shared)
            m["xT"] = cur[b]
            in_maps.append(m)
        res = run_bass_kernel_spmd(nc, in_maps, core_ids=list(range(B)))
        cur = [np.asarray(res.results[b]["yT"]) for b in range(B)]
        del shared, in_maps
    out = np.stack([c.T for c in cur], axis=0)
    return np.ascontiguousarray(out.astype(np.float32))
```
